# Optimizing a Trainium2 kernel written in Bass

```python
import jax, jax.numpy as jnp
from jax import lax
import numpy as np

D_MODEL = 2048
BATCH = 1
SEQ = 8192
DEPTH = 2

CTX_LEN = 256
GRID_W = 64
D_MIX = D_MODEL
D_CONV = D_MIX // 2
D_RET = D_MIX - D_CONV
RET_HEADS = 8
RET_HEAD_DIM = D_RET // RET_HEADS
CONV_WIDTH = 3
CHUNK = 128
D_IN = 4 * D_CONV + 4 * D_RET
ROPE_BASE = 10000.0
RET_DECAY_OFFSET = 5.0
EPS = 1e-6

kernel_name = "hybrid_conv_retention_prefix_dit_block"


def _rmsnorm(x, w):
    xf = x.astype(jnp.float32)
    y = xf * lax.rsqrt(jnp.mean(xf * xf, axis=-1, keepdims=True) + EPS)
    return (y * w.astype(jnp.float32)).astype(x.dtype)


def _split_in(u):
    idx = [D_CONV, 2 * D_CONV, 3 * D_CONV, 4 * D_CONV,
           4 * D_CONV + D_RET, 4 * D_CONV + 2 * D_RET, 4 * D_CONV + 3 * D_RET]
    return jnp.split(u, idx, axis=-1)


def _short_conv(u, w):
    up = jnp.pad(u, ((0, 0), (1, 1), (0, 0)))
    return up[:, :-2] * w[0] + up[:, 1:-1] * w[1] + up[:, 2:] * w[2]


def _conv_branch(h, b, c, z, conv_w, norm_w):
    y = b * _short_conv(c * h, conv_w)
    return jax.nn.silu(z) * _rmsnorm(y, norm_w)


def _heads(t):
    b, l, _ = t.shape
    return t.reshape(b, l, RET_HEADS, RET_HEAD_DIM).transpose(0, 2, 1, 3)


def _rope_1d(x, pos):
    f = x.shape[-1] // 2
    inv = ROPE_BASE ** (-jnp.arange(f, dtype=jnp.float32) / f)
    ang = pos.astype(jnp.float32)[:, None] * inv[None, :]
    cos, sin = jnp.cos(ang), jnp.sin(ang)
    x1, x2 = x[..., :f], x[..., f:]
    return jnp.concatenate([x1 * cos - x2 * sin, x1 * sin + x2 * cos], axis=-1).astype(x.dtype)


def _axial_rope(x, row_pos, col_pos):
    half = x.shape[-1] // 2
    return jnp.concatenate([_rope_1d(x[..., :half], row_pos),
                            _rope_1d(x[..., half:], col_pos)], axis=-1)


def _chunk_retention(q, k, v, lg, s0):
    b, h, l, dk = q.shape
    n = l // CHUNK
    qc = q.reshape(b, h, n, CHUNK, dk)
    kc = k.reshape(b, h, n, CHUNK, dk)
    vc = v.reshape(b, h, n, CHUNK, v.shape[-1])
    pos = jnp.arange(CHUNK, dtype=jnp.float32)
    diff = pos[:, None] - pos[None, :]
    dmask = jnp.where(diff >= 0, jnp.exp(lg[:, None, None] * jnp.maximum(diff, 0.0)[None]), 0.0)
    scores = jnp.einsum('bhnid,bhnjd->bhnij', qc, kc) * dmask[None, :, None]
    intra = jnp.einsum('bhnij,bhnje->bhnie', scores, vc)
    k_decay = jnp.exp(lg[:, None] * (CHUNK - 1 - pos)[None])
    q_decay = jnp.exp(lg[:, None] * (pos + 1.0)[None])
    chunk_decay = jnp.exp(lg * CHUNK)
    chunk_kv = jnp.einsum('bhnjd,hj,bhnje->nbhde', kc, k_decay, vc)

    def step(s, kv):
        return chunk_decay[None, :, None, None] * s + kv, s

    _, s_prev = lax.scan(step, s0, chunk_kv)
    inter = jnp.einsum('bhnid,hi,nbhde->bhnie', qc, q_decay, s_prev)
    return (intra + inter).reshape(b, h, l, -1)


def _bidir_retention(q, k, v, lg_f, lg_b, s0_f, s0_b):
    o_f = _chunk_retention(q, k, v, lg_f, s0_f)
    flip = lambda t: jnp.flip(t, axis=2)
    o_b = _chunk_retention(flip(q), flip(k), flip(v), lg_b, s0_b)
    return o_f + flip(o_b)


def _context_states(k, v, lg_f, lg_b):
    lc = k.shape[2]
    t = jnp.arange(lc, dtype=jnp.float32)
    w_f = jnp.exp(lg_f[:, None] * (lc - 1.0 - t)[None])
    w_b = jnp.exp(lg_b[:, None] * t[None])
    s_f = jnp.einsum('bhtd,ht,bhte->bhde', k, w_f, v)
    s_b = jnp.einsum('bhtd,ht,bhte->bhde', k, w_b, v)
    return s_f, s_b


def _ret_out(o, z, gn_w):
    of = o.astype(jnp.float32)
    mu = jnp.mean(of, axis=-1, keepdims=True)
    var = jnp.mean(jnp.square(of - mu), axis=-1, keepdims=True)
    on = (of - mu) * lax.rsqrt(var + EPS)
    b, h, l, d = on.shape
    on = on.transpose(0, 2, 1, 3).reshape(b, l, h * d) * gn_w.astype(jnp.float32)
    return jax.nn.silu(z) * on.astype(z.dtype)


def _layer(x, ctx, c, c_ctx, norm_w, w_mod, b_mod, w_in, conv_w, conv_norm_w, ret_norm_w,
           decay_f, decay_b, w_out, row_pos, col_pos, update_ctx):
    d = D_MODEL
    lg_f = -jnp.exp(decay_f.astype(jnp.float32))
    lg_b = -jnp.exp(decay_b.astype(jnp.float32))
    k_scale = RET_HEAD_DIM ** -0.5

    shift, scale, gate = jnp.split(jax.nn.silu(c) @ w_mod + b_mod, 3, axis=-1)
    hx = _rmsnorm(x, norm_w) * (1 + scale[:, None]) + shift[:, None]
    a_h, a_b, a_c, a_z, q, k, v, r_z = _split_in(hx @ w_in)

    n_mod = 3 if update_ctx else 2
    mod_c = jax.nn.silu(c_ctx) @ w_mod[:, :n_mod * d] + b_mod[:n_mod * d]
    hc = _rmsnorm(ctx, norm_w) * (1 + mod_c[d:2 * d]) + mod_c[:d]
    if update_ctx:
        ca_h, ca_b, ca_c, ca_z, cq, ck, cv, cr_z = _split_in(hc @ w_in)
    else:
        kv0 = 4 * D_CONV + D_RET
        ck, cv = jnp.split(hc @ w_in[:, kv0:kv0 + 2 * D_RET], 2, axis=-1)
    ck_h = _heads(ck) * k_scale
    cv_h = _heads(cv)
    s_f, s_b = _context_states(ck_h, cv_h, lg_f, lg_b)

    q_h = _axial_rope(_heads(q), row_pos, col_pos)
    k_h = _axial_rope(_heads(k), row_pos, col_pos) * k_scale
    o = _bidir_retention(q_h, k_h, _heads(v), lg_f, lg_b, s_f, s_b)
    y_ret = _ret_out(o, r_z, ret_norm_w)
    y_conv = _conv_branch(a_h, a_b, a_c, a_z, conv_w, conv_norm_w)
    x = x + gate[:, None] * (jnp.concatenate([y_conv, y_ret], axis=-1) @ w_out)

    if update_ctx:
        zeros = jnp.zeros_like(s_f)
        oc = _bidir_retention(_heads(cq), ck_h, cv_h, lg_f, lg_b, zeros, zeros)
        yc_ret = _ret_out(oc, cr_z, ret_norm_w)
        yc_conv = _conv_branch(ca_h, ca_b, ca_c, ca_z, conv_w, conv_norm_w)
        ctx = ctx + mod_c[2 * d:] * (jnp.concatenate([yc_conv, yc_ret], axis=-1) @ w_out)
    return x, ctx


def setup_inputs(seed: int = 0) -> dict:
    key = jax.random.key(seed)
    ks = jax.random.split(key, 16)
    f32 = jnp.float32
    nrm = lambda k, s: jax.random.normal(k, s, f32)
    base = jnp.log(-jnp.log1p(-(2.0 ** -(RET_DECAY_OFFSET + jnp.arange(RET_HEADS, dtype=f32)))))
    return {
        "x": nrm(ks[0], (BATCH, SEQ, D_MODEL)),
        "c": nrm(ks[1], (BATCH, D_MODEL)),
        "ctx": nrm(ks[2], (BATCH, CTX_LEN, D_MODEL)),
        "c_ctx": nrm(ks[3], (D_MODEL,)),
        "norm_w": 1.0 + 0.05 * nrm(ks[4], (DEPTH, D_MODEL)),
        "w_mod": nrm(ks[5], (DEPTH, D_MODEL, 3 * D_MODEL)) * (0.5 * D_MODEL ** -0.5),
        "b_mod": 0.02 * nrm(ks[6], (DEPTH, 3 * D_MODEL)),
        "w_in": nrm(ks[7], (DEPTH, D_MODEL, D_IN)) * D_MODEL ** -0.5,
        "conv_w": nrm(ks[8], (DEPTH, CONV_WIDTH, D_CONV)) * CONV_WIDTH ** -0.5,
        "conv_norm_w": 1.0 + 0.05 * nrm(ks[9], (DEPTH, D_CONV)),
        "ret_norm_w": 1.0 + 0.05 * nrm(ks[10], (DEPTH, D_RET)),
        "ret_decay_f": base[None] + 0.05 * nrm(ks[11], (DEPTH, RET_HEADS)),
        "ret_decay_b": base[None] + 0.05 * nrm(ks[12], (DEPTH, RET_HEADS)),
        "w_out": nrm(ks[13], (DEPTH, D_MIX, D_MODEL)) * D_MIX ** -0.5,
        "final_norm_w": 1.0 + 0.05 * nrm(ks[14], (D_MODEL,)),
    }


def reference(x, c, ctx, c_ctx, norm_w, w_mod, b_mod, w_in, conv_w, conv_norm_w, ret_norm_w,
              ret_decay_f, ret_decay_b, w_out, final_norm_w):
    seq = x.shape[1]
    rows = seq // GRID_W
    row_pos = jnp.repeat(jnp.arange(rows), GRID_W)
    col_pos = jnp.tile(jnp.arange(GRID_W), rows)
    for layer in range(DEPTH):
        x, ctx = _layer(x, ctx, c, c_ctx, norm_w[layer], w_mod[layer], b_mod[layer], w_in[layer],
                        conv_w[layer], conv_norm_w[layer], ret_norm_w[layer],
                        ret_decay_f[layer], ret_decay_b[layer], w_out[layer],
                        row_pos, col_pos, layer < DEPTH - 1)
    return _rmsnorm(x, final_norm_w)
```

```python
import numpy as np
import concourse.bass as bass
import concourse.mybir as mybir
from concourse.bass_utils import run_bass_kernel_spmd

F32 = mybir.dt.float32
F32R = mybir.dt.float32r
BF16 = mybir.dt.bfloat16
ALU = mybir.AluOpType
AF = mybir.ActivationFunctionType
AX = mybir.AxisListType

NCORE = 8
L = 2
D = 2048
KC = 16
T = 1024
TC = 256
TT = T + TC
NT = TT // 128
H = 8
EPS = 1e-6
RANGES = ((0, 512, 0), (512, 1024, 0), (1024, 1280, 1))
XW = 2064

DBG = False
MODE = "host"

ENGS = ("pe", "act", "dve", "pool", "sp")


class Unit:
    __slots__ = ("name", "last_w", "readers")

    def __init__(self, name):
        self.name = name
        self.last_w = None
        self.readers = {}


class Sched:
    NDMA = 48
    NDMA_SP = 40

    def __init__(self, nc):
        self.nc = nc
        self.streams = {e: [] for e in ENGS}
        self.esem = {e: nc.alloc_semaphore("s_" + e) for e in ENGS if e != "sp"}
        self.ecnt = {e: 0 for e in ENGS}
        self.dsem = [nc.alloc_semaphore("d%d" % i) for i in range(self.NDMA)]
        self.dcnt = [0] * self.NDMA
        self.dnext = 0
        self.pnext = 0
        self.waited = {e: {} for e in ENGS}

    def _sem(self, key):
        return self.esem[key[1]] if key[0] == "e" else self.dsem[key[1]]

    def _need(self, eng, tok, need):
        if tok is None:
            return
        key, val, src = tok
        if src == "pe" and eng == "pe":
            return
        if need.get(key, 0) < val:
            need[key] = val

    def _emit_waits(self, eng, need):
        w = self.waited[eng]
        for key, val in need.items():
            if w.get(key, 0) >= val:
                continue
            w[key] = val
            self.streams[eng].append(("wait", self._sem(key), val))

    @staticmethod
    def _addr(u, tok):
        o = u.readers.get(tok[0])
        if o is None or o[1] < tok[1]:
            u.readers[tok[0]] = tok

    def _deps(self, eng, reads, writes):
        need = {}
        for u in reads:
            self._need(eng, u.last_w, need)
        for u in writes:
            self._need(eng, u.last_w, need)
            for t in u.readers.values():
                self._need(eng, t, need)
        return need

    def _commit(self, tok, reads, writes):
        for u in reads:
            self._addr(u, tok)
        for u in writes:
            u.last_w = tok
            u.readers = {}

    def op(self, eng, fn, reads=(), writes=(), dma=None, inc=16):
        if dma is None:
            dma = (eng == "sp")
        need = self._deps(eng, reads, writes)
        if dma:
            if eng == "pool":
                idx = self.NDMA_SP + self.pnext
                self.pnext = (self.pnext + 1) % (self.NDMA - self.NDMA_SP)
            else:
                idx = self.dnext
                self.dnext = (self.dnext + 1) % self.NDMA_SP
            key = ("d", idx)
            if self.dcnt[idx] > 0 and need.get(key, 0) < self.dcnt[idx]:
                need[key] = self.dcnt[idx]
            self._emit_waits(eng, need)
            self.dcnt[idx] += inc
            tok = (key, self.dcnt[idx], "dma")
            self.streams[eng].append(("ins", fn, self.dsem[idx], inc))
        else:
            self._emit_waits(eng, need)
            self.ecnt[eng] += 1
            tok = (("e", eng), self.ecnt[eng], eng)
            self.streams[eng].append(("ins", fn, self.esem[eng], 1))
        self._commit(tok, reads, writes)
        return tok

    def mm(self, fn, reads=(), writes=(), inc=True):
        if inc:
            return self.op("pe", fn, reads, writes)
        need = self._deps("pe", reads, writes)
        self._emit_waits("pe", need)
        self.streams["pe"].append(("ins0", fn))
        tok = (("e", "pe"), self.ecnt["pe"] + 1, "pe")
        self._commit(tok, reads, writes)
        return tok

    def final_wait_all(self):
        need = {}
        for i in range(self.NDMA):
            if self.dcnt[i] > 0:
                need[("d", i)] = self.dcnt[i]
        for e in self.esem:
            if self.ecnt[e] > 0:
                need[("e", e)] = self.ecnt[e]
        self._emit_waits("sp", need)

    def replay(self):
        nc = self.nc
        with nc.Block() as block:
            def run(stream):
                def f(eng):
                    for it in stream:
                        if it[0] == "wait":
                            eng.wait_ge(it[1], it[2])
                        elif it[0] == "ins":
                            it[1](eng).then_inc(it[2], it[3])
                        else:
                            it[1](eng)
                return f
            block.tensor(run(self.streams["pe"]))
            block.scalar(run(self.streams["act"]))
            block.vector(run(self.streams["dve"]))
            block.gpsimd(run(self.streams["pool"]))
            block.sync(run(self.streams["sp"]))


def I(method, *a, **kw):
    return lambda e: getattr(e, method)(*a, **kw)


def AP(t, off, dims):
    return bass.AP(t, off, [list(d) for d in dims])


C_NW, C_BMOD, C_CONVW, C_CNW, C_RNW, C_DECF, C_DECB = 0, 16, 64, 88, 96, 104, 112
CL = 120
C_FNW = L * CL
NCST = L * CL + 16
A_EAF, A_EAB, A_KDF, A_KDB = 0, 8, 16, 17
A_EXC, A_MSK = 18, 36
A_SELP, A_SELN = 54, 62
A_QDF, A_QDB = 70, 198
NPCA = 326
B_MRF, B_MRB, B_MU, B_ML, B_MI = 0, 128, 256, 384, 512
B_ID = 640
B_ROPE = 768
NPCB = B_ROPE + NT * 192

N_INBLK = 32
BLOCKS = ([("k", b) for b in range(4)] + [("v", b) for b in range(4)] +
          [x for cc in range(8) for x in (("hc", cc), ("bz", cc))] +
          [("q", b) for b in range(4)] + [("zr", b) for b in range(4)])


def _block_cols(kind, i):
    r128 = np.arange(128)
    r256 = np.arange(256)
    if kind in ("k", "q"):
        base = (5 if kind == "k" else 4) * 1024 + i * 256
        hf, ii, hh, ss = np.meshgrid(np.arange(2), np.arange(32), np.arange(2), np.arange(2), indexing="ij")
        return (base + hh * 128 + hf * 64 + ss * 32 + ii).reshape(-1)
    if kind == "v":
        return 6 * 1024 + i * 256 + r256
    if kind == "zr":
        return 7 * 1024 + i * 256 + r256
    if kind == "hc":
        return np.concatenate([0 * 1024 + i * 128 + r128, 2 * 1024 + i * 128 + r128])
    if kind == "bz":
        return np.concatenate([1 * 1024 + i * 128 + r128, 3 * 1024 + i * 128 + r128])
    raise ValueError(kind)


def build_program(stop):
    nc = bass.Bass("TRN2", target_bir_lowering=False)
    S = Sched(nc)
    sp, act, dve, pool = "sp", "act", "dve", "pool"

    def dram_in(name, shape, dt=F32):
        return nc.dram_tensor(name, list(shape), dt, kind="ExternalInput").ap()

    xT_d = dram_in("xT", [D, TT])
    c2_d = dram_in("c2", [128, KC * 2])
    cst_d = dram_in("cst", [128, NCST])
    pca_d = dram_in("pca", [128, NPCA])
    pcb_d = dram_in("pcb", [128, NPCB])
    wmod_d = dram_in("wmod", [L, 24, 128, KC * 256])
    win_d = dram_in("win", [L, N_INBLK, 128, KC * 256])
    wout_d = dram_in("wout", [L, 2, 8, 128, 8 * 256])
    n_ex = L if stop is None else stop
    gin_d = [dram_in("gin%d" % e, [NCORE, 128, XW]) for e in range(n_ex)] if MODE == "host" else []
    if stop is None:
        out_d = nc.dram_tensor("outT", [D, T], F32, kind="ExternalOutput").ap()
    else:
        xch_out_d = nc.dram_tensor("xch", [128, XW], F32, kind="ExternalOutput").ap()

    def scratch(name, shape, dt=BF16):
        return nc.dram_tensor(name, list(shape), dt).ap()
    QT_d = scratch("QTs", [H, 128, TT]); uQT = [Unit("QT%d" % h) for h in range(H)]
    KT_d = scratch("KTs", [H, 128, TT]); uKT = [Unit("KT%d" % h) for h in range(H)]
    KTOK_d = scratch("KTOKs", [NT, 128, 1024]); uKTOK = [Unit("KTOK%d" % t) for t in range(NT)]
    V_d = scratch("Vs", [NT, 128, 1024]); uV = [Unit("V%d" % t) for t in range(NT)]
    UF_d = scratch("UFs", [5, 8, 128, TT]); uUF = [[Unit("UF%d_%d" % (g, c)) for c in range(8)] for g in range(5)]
    SBS_d = scratch("SBSs", [NT, 128, 1024]); uSBSd = [Unit("SBSd%d" % t) for t in range(NT)]
    if MODE == "cc":
        xch_loc_d = [scratch("xchl%d" % e, [128, XW], F32) for e in range(L)]
        xch_all_d = [scratch("xcha%d" % e, [NCORE * 128, XW], F32) for e in range(L)]
    uXCH = [Unit("xch%d" % e) for e in range(L)]

    def sb(name, shape, dt=F32):
        return nc.alloc_sbuf_tensor(name, list(shape), dt), Unit(name)

    XT = [sb("XT%d" % k, [128, TT]) for k in range(KC)]
    NWB = 3
    WB = [sb("WB%d" % i, [128, KC * 256], BF16) for i in range(NWB)]
    CST, uCST = sb("CST", [128, NCST])
    PCA, uPCA = sb("PCA", [128, NPCA])
    SC, uSC = sb("SC", [128, KC * 2], BF16)
    SCF, uSCF = sb("SCF", [128, KC * 2])
    ONES, uONES = sb("ONES", [128, 128], F32R)
    ONEF, uONEF = sb("ONEF", [128, 128])
    ONESM, uONESM = sb("ONESM", [128, 128], F32R)
    IDB, uIDB = sb("IDB", [128, 128], BF16)
    I2, uI2 = sb("I2", [2, 2])
    EPSC, uEPSC = sb("EPSC", [128, 1])
    MODTS = [sb("MODT%d" % l_, [128, 96]) for l_ in range(L)]
    MVECS = [sb("MVEC%d" % l_, [128, 6 * 16]) for l_ in range(L)]
    LG, uLG = sb("LG", [128, 16])
    RSTD, uRSTD = sb("RSTD", [128, TT])
    MR, uMR = sb("MR", [2, 256])
    COEF, uCOEF = sb("COEF", [128, 2 * 9 * 8])
    DAF, uDAF = sb("DAF", [128, 64])
    DAB, uDAB = sb("DAB", [128, 64])
    KDEC, uKDEC = sb("KDEC", [128, 16])
    CDEC, uCDEC = sb("CDEC", [128, 16])
    QDF, uQDF = sb("QDF", [128, 1024])
    QDB, uQDB = sb("QDB", [128, 1024])
    MTB, uMTB = sb("MTB", [128, 1024])
    HALO, uHALO = sb("HALO", [128, 16])
    HBND, uHBND = sb("HBND", [128, 16])
    HG, uHG = sb("HG", [128, NCORE * 16])
    SML, uSML = sb("SML", [128, 64])
    BAR, uBAR = sb("BAR", [128, 1])
    LNKS, uLNKS = sb("LNKS", [128, 1])
    SQR = [sb("SQR%d" % i, [128, 512], F32R) for i in range(2)]
    GN1, uGN1 = sb("GN1", [128, 1024], F32R)
    GN3, uGN3 = sb("GN3", [128, 1024], F32R)

    HW = (nc.sbuf_bytes_remaining - 256) // 4 // 16 * 16
    HREG = nc.alloc_sbuf_tensor("HREG", [128, HW], F32)
    v_f32 = HREG[:]
    v_b16 = HREG[:].bitcast(BF16)
    v_f32r = HREG[:].bitcast(F32R)

    def RF(off, dims):
        return bass.AP(v_f32.tensor, off, [[v_f32.ap[0][0], 128]] + [list(d) for d in dims])

    def RB(off, dims):
        return bass.AP(v_b16.tensor, off, [[v_b16.ap[0][0], 128]] + [list(d) for d in dims])

    def RR(off, dims):
        return bass.AP(v_f32r.tensor, off, [[v_f32r.ap[0][0], 128]] + [list(d) for d in dims])

    reg = {"off": 0, "units": []}

    def r_reset():
        reg["off"] = 0
        old = reg["units"]
        reg["units"] = []
        return old

    def r_alloc(nbytes, name):
        o = reg["off"]
        reg["off"] += ((nbytes + 63) // 64) * 64
        assert reg["off"] <= HW * 4, ("phase region overflow", name, reg["off"], HW * 4)
        u = Unit(name)
        reg["units"].append(u)
        return o, u

    def r_b16(n, name):
        o, u = r_alloc(n * 2, name)
        return o // 2, u

    def r_f32(n, name):
        o, u = r_alloc(n * 4, name)
        return o // 4, u

    def barrier(old_units, new_units):
        S.op(pool, I("memset", BAR[:], 0.0), [], list(old_units) + list(new_units) + [uBAR])

    PB = []
    for i in range(8):
        PB.append((nc.alloc_psum_tensor("PB%d" % i, [128, 512], F32), Unit("PB%d" % i)))

    def dma(out, in_, reads, writes):
        S.op(sp, I("dma_start", out=out, in_=in_), reads, writes)

    for k in range(KC):
        dma(XT[k][0][:], xT_d[k * 128:(k + 1) * 128, :], [], [XT[k][1]])
    dma(CST[:], cst_d, [], [uCST])
    dma(PCA[:], pca_d, [], [uPCA])
    dma(SCF[:], c2_d, [], [uSCF])
    dma(I2[:], pcb_d[0:2, B_ID:B_ID + 2], [], [uI2])
    S.op(act, I("activation", out=SC[:], in_=SCF[:], func=AF.Silu), [uSCF], [uSC])
    S.op(pool, I("memset", ONEF[:], 1.0), [], [uONEF])
    S.op(act, I("activation", out=ONES[:], in_=ONEF[:], func=AF.Identity), [uONEF], [uONES])
    S.op(act, I("activation", out=ONESM[:], in_=ONEF[:], func=AF.Identity, scale=1.0 / 128.0), [uONEF], [uONESM])
    S.op(pool, I("memset", EPSC[:], EPS), [], [uEPSC])
    S.op(pool, I("memset", LNKS[:], float(np.log(128.0 ** -0.5))), [], [uLNKS])

    wctr = {"n": 0}

    def wload(src, ncols):
        i = wctr["n"] % NWB
        wctr["n"] += 1
        wt, uwt = WB[i]
        S.op(pool, I("dma_start", out=wt[:, 0:ncols], in_=src), [], [uwt], dma=True)
        return WB[i]

    pbrot = {"n": 0}

    def next_pb(lo=0, hi=6):
        i = lo + pbrot["n"] % (hi - lo)
        pbrot["n"] += 1
        return PB[i]

    def mvec(j, which):
        return (which * 2 + j) * 16

    wsrcs = []
    wsrcs += [(wmod_d[0, b], KC * 256) for b in range(24)]
    for l in range(L):
        wsrcs += [(win_d[l, b], KC * 256) for b in range(N_INBLK)]
        if l + 1 < L:
            wsrcs += [(wmod_d[l + 1, b], KC * 256) for b in range(24)]
        wsrcs += [(wout_d[l, 1, b], 8 * 256) for b in range(8)]
        wsrcs += [(wout_d[l, 0, b], 8 * 256) for b in range(8)]
    wpos = {"n": 0}
    wq = []

    def w_next():
        while len(wq) < NWB - 1 and wpos["n"] < len(wsrcs):
            src, ncols = wsrcs[wpos["n"]]
            wpos["n"] += 1
            wq.append(wload(src, ncols))
        cur = wq.pop(0)
        while len(wq) < NWB - 1 and wpos["n"] < len(wsrcs):
            src, ncols = wsrcs[wpos["n"]]
            wpos["n"] += 1
            wq.append(wload(src, ncols))
        return cur

    RBANKS = (PB[5], PB[6], PB[7])

    def rms_k(k):
        for ri, (t0, t1, _) in enumerate(RANGES):
            (sq, usq) = SQR[(k * 3 + ri) % 2]
            n = t1 - t0
            S.op(act, I("activation", out=sq[:, 0:n], in_=XT[k][0][:, t0:t1], func=AF.Square),
                 [XT[k][1]], [usq])
            S.mm(I("matmul", RBANKS[ri][0][:, 0:n], ONES[:], sq[:, 0:n], start=(k == 0), stop=(k == KC - 1)),
                 [uONES, usq], [RBANKS[ri][1]], inc=True)

    def rms_finish():
        for ri, (t0, t1, _) in enumerate(RANGES):
            n = t1 - t0
            S.op(act, I("activation", out=RSTD[:, t0:t1], in_=RBANKS[ri][0][:, 0:n], func=AF.Ln, bias=EPSC[:, 0:1], scale=1.0 / D),
                 [RBANKS[ri][1], uEPSC], [uRSTD])
        S.op(act, I("activation", out=RSTD[:], in_=RSTD[:], func=AF.Exp, scale=-0.5), [uRSTD], [uRSTD])

    def mod_block(lm, b):
        MODT, uMODT = MODTS[lm]
        wt, uwt = w_next()
        pb, upb = PB[0]
        for k in range(KC):
            S.mm(I("matmul", pb[0:2, 0:256], SC[:, 2 * k:2 * k + 2], wt[:, k * 256:(k + 1) * 256], start=(k == 0), stop=(k == KC - 1)),
                 [uSC, uwt], [upb], inc=(k == KC - 1))
        S.op(act, I("activation", out=MR[:], in_=pb[0:2, 0:256], func=AF.Copy), [upb], [uMR])
        pb2, upb2 = PB[1]
        for hh in range(2):
            S.mm(I("matmul", pb2[:, 2 * hh:2 * hh + 2], MR[0:2, hh * 128:(hh + 1) * 128], I2[:], start=True, stop=True),
                 [uMR, uI2], [upb2], inc=True)
        S.op(dve, I("tensor_copy", out=MODT[:, 4 * b:4 * b + 4], in_=pb2[:, 0:4]), [upb2], [uMODT])

    def mod_finish(lm):
        MODT, uMODT = MODTS[lm]
        MVEC, uMVEC = MVECS[lm]
        cbm = lm * CL
        S.op(dve, I("tensor_tensor", out=AP(MODT, 0, [[96, 128], [1, 2], [2, 48]]), in0=AP(MODT, 0, [[96, 128], [1, 2], [2, 48]]),
                    in1=AP(CST, cbm + C_BMOD, [[NCST, 128], [0, 2], [1, 48]]), op=ALU.add),
             [uMODT, uCST], [uMODT])
        for j in range(2):
            shift = AP(MODT, j, [[96, 128], [2, 16]])
            scale = AP(MODT, 32 + j, [[96, 128], [2, 16]])
            gate = AP(MODT, 64 + j, [[96, 128], [2, 16]])
            S.op(dve, I("scalar_tensor_tensor", out=MVEC[:, mvec(j, 0):mvec(j, 0) + 16], in0=scale, scalar=1.0, in1=CST[:, cbm + C_NW:cbm + C_NW + 16], op0=ALU.add, op1=ALU.mult),
                 [uMODT, uCST], [uMVEC])
            S.op(dve, I("tensor_copy", out=MVEC[:, mvec(j, 1):mvec(j, 1) + 16], in_=shift), [uMODT], [uMVEC])
            S.op(dve, I("tensor_copy", out=MVEC[:, mvec(j, 2):mvec(j, 2) + 16], in_=gate), [uMODT], [uMVEC])

    modq = {"l": 0, "b": 0}

    def mod_emit(n):
        for _ in range(n):
            if modq["l"] >= L or modq["b"] >= 24:
                return
            mod_block(modq["l"], modq["b"])
            modq["b"] += 1
            if modq["b"] == 24:
                mod_finish(modq["l"])

    completed = True
    for l in range(L):
        cb = l * CL
        old = r_reset()
        HX = [r_b16(TT, "HX%d" % k) for k in range(KC)]
        PCB_o, uPCB = r_f32(NPCB, "PCB")
        STG = [r_b16(512, "STG%d" % i) for i in range(4)]
        STT = [r_b16(TT, "STT%d" % i) for i in range(2)]
        RTA_o, uRTA = r_f32(256, "RTA")
        RTB_o, uRTB = r_f32(256, "RTB")
        SQT = [r_f32(512, "TMPA%d" % i) for i in range(2)]
        barrier(old, reg["units"])
        dma(RF(PCB_o, [[1, NPCB]]), pcb_d, [], [uPCB])
        if l == 0:
            S.op(dve, I("tensor_copy", out=IDB[:], in_=RF(PCB_o + B_ID, [[1, 128]])), [uPCB], [uIDB])

        def PCBc(col, dims, PCB_o=PCB_o):
            return RF(PCB_o + col, dims)

        if l == 0:
            for k_ in range(KC):
                rms_k(k_)
            mod_emit(24)
        rms_finish()
        assert modq["l"] == l and modq["b"] == 24
        MVEC, uMVEC = MVECS[l]
        modq["l"], modq["b"] = l + 1, 0
        S.op(act, I("activation", out=LG[:], in_=CST[:, cb + C_DECF:cb + C_DECF + 16], func=AF.Exp), [uCST], [uLG])
        S.op(dve, I("tensor_scalar_mul", out=LG[:], in0=LG[:], scalar1=-1.0), [uLG], [uLG])
        for d_ in range(2):
            dst, udst = (DAF, uDAF) if d_ == 0 else (DAB, uDAB)
            pcol = A_EAF if d_ == 0 else A_EAB
            S.op(dve, I("tensor_tensor", out=AP(dst, 0, [[64, 128], [8, 8], [1, 8]]),
                                                                   in0=AP(PCA, pcol, [[NPCA, 128], [1, 8], [0, 8]]),
                                                                   in1=AP(LG, 8 * d_, [[16, 128], [0, 8], [1, 8]]), op=ALU.mult),
                 [uPCA, uLG], [udst])
            S.op(act, I("activation", out=dst[:], in_=dst[:], func=AF.Exp), [udst], [udst])
            kcol = A_KDF if d_ == 0 else A_KDB
            S.op(dve, I("tensor_scalar_mul", out=KDEC[:, 8 * d_:8 * d_ + 8], in0=LG[:, 8 * d_:8 * d_ + 8], scalar1=PCA[:, kcol:kcol + 1]),
                 [uLG, uPCA], [uKDEC])
            S.op(dve, I("tensor_scalar_mul", out=CDEC[:, 8 * d_:8 * d_ + 8], in0=LG[:, 8 * d_:8 * d_ + 8], scalar1=128.0), [uLG], [uCDEC])
        S.op(act, I("activation", out=KDEC[:], in_=KDEC[:], func=AF.Exp), [uKDEC], [uKDEC])
        S.op(act, I("activation", out=CDEC[:], in_=CDEC[:], func=AF.Exp), [uCDEC], [uCDEC])
        rta = RF(RTA_o, [[1, 128]])
        rtb = RF(RTB_o, [[1, 128]])
        for h in range(H):
            hs = slice(h * 128, (h + 1) * 128)
            S.op(act, I("activation", out=QDF[:, hs], in_=PCA[:, A_QDF:A_QDF + 128], func=AF.Exp, scale=LG[:, h:h + 1], bias=LNKS[:, 0:1]), [uPCA, uLG, uLNKS], [uQDF])
            S.op(act, I("activation", out=QDB[:, hs], in_=PCA[:, A_QDB:A_QDB + 128], func=AF.Exp, scale=LG[:, 8 + h:9 + h], bias=LNKS[:, 0:1]), [uPCA, uLG, uLNKS], [uQDB])
            S.op(act, I("activation", out=rta, in_=PCBc(B_MRF, [[1, 128]]), func=AF.Exp, scale=LG[:, h:h + 1]), [uPCB, uLG], [uRTA])
            S.op(act, I("activation", out=rtb, in_=PCBc(B_MRB, [[1, 128]]), func=AF.Exp, scale=LG[:, 8 + h:9 + h]), [uPCB, uLG], [uRTB])
            S.op(dve, I("tensor_tensor", out=rta, in0=rta, in1=PCBc(B_MU, [[1, 128]]), op=ALU.mult), [uRTA, uPCB], [uRTA])
            S.op(dve, I("tensor_tensor", out=rtb, in0=rtb, in1=PCBc(B_ML, [[1, 128]]), op=ALU.mult), [uRTB, uPCB], [uRTB])
            S.op(dve, I("tensor_tensor", out=rta, in0=rta, in1=rtb, op=ALU.add), [uRTA, uRTB], [uRTA])
            S.op(dve, I("tensor_tensor", out=MTB[:, hs], in0=rta, in1=PCBc(B_MI, [[1, 128]]), op=ALU.add), [uRTA, uPCB], [uMTB])
        for d_ in range(2):
            S.op(dve, I("tensor_tensor", out=AP(COEF, d_ * 72, [[144, 128], [8, 9], [1, 8]]),
                                                       in0=AP(PCA, A_EXC + 9 * d_, [[NPCA, 128], [1, 9], [0, 8]]),
                                                       in1=AP(LG, 8 * d_, [[16, 128], [0, 9], [1, 8]]), op=ALU.mult), [uPCA, uLG], [uCOEF])
        S.op(act, I("activation", out=COEF[:], in_=COEF[:], func=AF.Exp), [uCOEF], [uCOEF])
        S.op(dve, I("tensor_tensor", out=AP(COEF, 0, [[144, 128], [8, 18], [1, 8]]), in0=AP(COEF, 0, [[144, 128], [8, 18], [1, 8]]),
                                            in1=AP(PCA, A_MSK, [[NPCA, 128], [1, 18], [0, 8]]), op=ALU.mult), [uCOEF, uPCA], [uCOEF])

        for k in range(KC):
            for ri, (t0, t1, j) in enumerate(RANGES):
                n = t1 - t0
                (sqo, usq) = SQT[(k * 3 + ri) % 2]
                S.op(dve, I("tensor_tensor", out=RF(sqo, [[1, n]]), in0=XT[k][0][:, t0:t1], in1=RSTD[:, t0:t1], op=ALU.mult),
                     [XT[k][1], uRSTD], [usq])
                S.op(act, I("activation", out=RB(HX[k][0] + t0, [[1, n]]), in_=RF(sqo, [[1, n]]), func=AF.Identity,
                                                                                    bias=MVEC[:, mvec(j, 1) + k:mvec(j, 1) + k + 1],
                                                                                    scale=MVEC[:, mvec(j, 0) + k:mvec(j, 0) + k + 1]),
                     [usq, uMVEC], [HX[k][1]])

        def HXv(k, t0, t1, HX=HX):
            return RB(HX[k][0] + t0, [[1, t1 - t0]])

        def tok_block(kind, bi, wt, uwt):
            pend = []
            for tt in range(NT if (l < L - 1 or kind != "q") else 8):
                pb, upb = next_pb()
                for k in range(KC):
                    S.mm(I("matmul", pb[:, 0:256], HXv(k, tt * 128, (tt + 1) * 128), wt[:, k * 256:(k + 1) * 256], start=(k == 0), stop=(k == KC - 1)),
                         [HX[k][1], uwt], [upb], inc=(k == KC - 1))
                (sto, ust) = STG[tt % 4]
                if kind == "v":
                    S.op(act, I("activation", out=RB(sto, [[1, 256]]), in_=pb[:, 0:256], func=AF.Copy), [upb], [ust])
                    dma(V_d[tt][:, bi * 256:(bi + 1) * 256], RB(sto, [[1, 256]]), [ust], [uV[tt]])
                    continue
                ro = B_ROPE + tt * 192
                xv = AP(pb, 0, [[512, 128], [4, 64], [1, 4]])
                xs = AP(pb, 1, [[512, 128], [4, 64], [2, 2], [-1, 2]])
                cc_ = PCBc(ro, [[1, 64], [0, 4]])
                ss_ = PCBc(ro + 64, [[2, 64], [0, 2], [1, 2]])
                S.op(dve, I("tensor_tensor", out=RF(RTA_o, [[4, 64], [1, 4]]), in0=xv, in1=cc_, op=ALU.mult), [upb, uPCB], [uRTA])
                S.op(dve, I("tensor_tensor", out=RF(RTB_o, [[4, 64], [2, 2], [1, 2]]), in0=xs, in1=ss_, op=ALU.mult), [upb, uPCB], [uRTB])
                S.op(pool, I("tensor_tensor", out=RB(sto, [[2, 64], [128, 2], [1, 2]]), in0=RF(RTA_o, [[4, 64], [2, 2], [1, 2]]), in1=RF(RTB_o, [[4, 64], [2, 2], [1, 2]]), op=ALU.add), [uRTA, uRTB], [ust])
                if kind == "k":
                    dma(KTOK_d[tt][:, bi * 256:(bi + 1) * 256], RB(sto, [[1, 256]]), [ust], [uKTOK[tt]])
                def do_tr(tt=tt, sto=sto, ust=ust):
                    ptb = PB[7][0][:].bitcast(BF16)
                    for hh in range(2):
                        S.mm(I("transpose", ptb[:, hh * 128:(hh + 1) * 128], RB(sto + hh * 128, [[1, 128]]), IDB[:]),
                             [ust, uIDB], [PB[7][1]], inc=True)
                        S.op(act, I("activation", out=RB(STT[hh][0] + tt * 128, [[1, 128]]), in_=ptb[:, hh * 128:(hh + 1) * 128], func=AF.Copy),
                             [PB[7][1]], [STT[hh][1]])
                pend.append(do_tr)
                if len(pend) > 1:
                    pend.pop(0)()
            for f_ in pend:
                f_()
            if kind != "v":
                dst_d, udst = (KT_d, uKT) if kind == "k" else (QT_d, uQT)
                for hh in range(2):
                    h = bi * 2 + hh
                    dma(dst_d[h], RB(STT[hh][0], [[1, TT]]), [STT[hh][1]], [udst[h]])

        fm_alt = {"n": 0}

        FRANGES = RANGES if l < L - 1 else RANGES[:2]

        def fm_group(wt, uwt, half):
            banks = (PB[0], PB[1], PB[2]) if fm_alt["n"] % 2 == 0 else (PB[3], PB[4], PB[5])
            fm_alt["n"] += 1
            for ri, (t0, t1, _) in enumerate(FRANGES):
                n = t1 - t0
                for k in range(KC):
                    S.mm(I("matmul", banks[ri][0][:, 0:n], wt[:, k * 256 + half * 128:k * 256 + half * 128 + 128], HXv(k, t0, t1), start=(k == 0), stop=(k == KC - 1)),
                         [HX[k][1], uwt], [banks[ri][1]], inc=(k == KC - 1))
            return banks

        evrot = {"n": 0}

        def fm_evac(banks, g, cc):
            outs = []
            for ri, (t0, t1, _) in enumerate(FRANGES):
                n = t1 - t0
                (sto, ust) = STG[evrot["n"] % 4]
                evrot["n"] += 1
                if g == 4:
                    S.op(act, I("activation", out=RB(sto, [[1, n]]), in_=banks[ri][0][:, 0:n], func=AF.Silu), [banks[ri][1]], [ust])
                elif evrot["n"] % 2 == 0:
                    S.op(act, I("activation", out=RB(sto, [[1, n]]), in_=banks[ri][0][:, 0:n], func=AF.Copy), [banks[ri][1]], [ust])
                else:
                    S.op(dve, I("tensor_copy", out=RB(sto, [[1, n]]), in_=banks[ri][0][:, 0:n]), [banks[ri][1]], [ust])
                dma(UF_d[g, cc][:, t0:t1], RB(sto, [[1, n]]), [ust], [uUF[g][cc]])
                outs.append((sto, ust))
            return outs

        for bi_, (kind, idx) in enumerate(BLOCKS):
            wt, uwt = w_next()
            if kind in ("q", "k", "v"):
                tok_block(kind, idx, wt, uwt)
            elif kind == "zr":
                for half in range(2):
                    banks = fm_group(wt, uwt, half)
                    fm_evac(banks, 4, idx * 2 + half)
            elif kind == "hc":
                cc = idx
                banks = fm_group(wt, uwt, 0)
                outs = fm_evac(banks, 0, cc)
                S.op(pool, I("tensor_copy", out=HBND[:, cc:cc + 1], in_=RB(outs[0][0], [[1, 1]])), [outs[0][1]], [uHBND])
                S.op(pool, I("tensor_copy", out=HBND[:, 8 + cc:9 + cc], in_=RB(outs[1][0] + 511, [[1, 1]])), [outs[1][1]], [uHBND])
                banks = fm_group(wt, uwt, 1)
                outs = fm_evac(banks, 1, cc)
                S.op(pool, I("tensor_tensor", out=HALO[:, cc:cc + 1], in0=RB(outs[0][0], [[1, 1]]), in1=HBND[:, cc:cc + 1], op=ALU.mult), [outs[0][1], uHBND], [uHALO])
                S.op(pool, I("tensor_tensor", out=HALO[:, 8 + cc:9 + cc], in0=RB(outs[1][0] + 511, [[1, 1]]), in1=HBND[:, 8 + cc:9 + cc], op=ALU.mult), [outs[1][1], uHBND], [uHALO])
            elif kind == "bz":
                cc = idx
                banks = fm_group(wt, uwt, 0)
                fm_evac(banks, 2, cc)
                banks = fm_group(wt, uwt, 1)
                fm_evac(banks, 3, cc)

        old = r_reset()
        YT = [r_b16(TT, "YT%d" % k) for k in range(8)]
        bKTOK = [r_b16(1024, "bKTOK%d" % i) for i in range(2)]
        bV = [r_b16(1024, "bV%d" % i) for i in range(2)]
        bQ = [r_b16(1024, "bQ%d" % i) for i in range(2)]
        bK = [r_b16(1024, "bK%d" % i) for i in range(2)]
        bGZ = [r_b16(1024, "bGZ%d" % i) for i in range(1)]
        bKS = r_b16(1024, "bKS")
        bPM = r_b16(1024, "bPM")
        bQF = r_b16(1024, "bQF")
        bQB = r_b16(1024, "bQB")
        bSFB = r_b16(1024, "bSFB")
        bSBS = [r_b16(1024, "bSBS%d" % i) for i in range(2)]
        bS = r_f32(1024, "bS")
        xs_mark = reg["off"]
        bT1 = r_f32(1024, "bT1")
        bT2 = r_f32(1024, "bT2")
        bPAD = r_f32(64, "bPAD")
        ret_end = reg["off"]
        XS_o = xs_mark // 4
        assert xs_mark + XW * 4 <= ret_end
        uXS = [bT1[1], bT2[1], bPAD[1]]
        barrier(old, reg["units"])

        def YTv(k, t0=0, t1=TT, YT=YT):
            return RB(YT[k][0] + t0, [[1, t1 - t0]])

        for tt in range(8):
            (ko, uk) = bKTOK[tt % 2]
            (vo, uv) = bV[tt % 2]
            dma(RB(ko, [[1, 1024]]), KTOK_d[tt], [uKTOK[tt]], [uk])
            dma(RB(vo, [[1, 1024]]), V_d[tt], [uV[tt]], [uv])
            for d_ in range(2):
                tab = DAF if d_ == 0 else DAB
                utab = uDAF if d_ == 0 else uDAB
                (kso, uks) = bKS if d_ == 0 else bPM
                S.op(dve, I("tensor_tensor", out=RB(kso, [[128, 8], [1, 128]]), in0=RB(ko, [[128, 8], [1, 128]]),
                                                 in1=AP(tab, tt * 8, [[64, 128], [1, 8], [0, 128]]), op=ALU.mult),
                     [uk, utab], [uks])
                for h in range(H):
                    pbk = PB[2 * d_ + h // 4]
                    S.mm(I("matmul", pbk[0][:, (h % 4) * 128:(h % 4) * 128 + 128], RB(kso + h * 128, [[1, 128]]), RB(vo + h * 128, [[1, 128]]), start=(tt == 0 and h % 4 == 0), stop=(tt == 7), skip_group_check=True),
                         [uks, uv], [pbk[1]], inc=(h % 4 == 3))
        for d_ in range(2):
            for hb in range(2):
                pbk = PB[2 * d_ + hb]
                S.op(act, I("activation", out=RF(XS_o + d_ * 1024 + hb * 512, [[1, 512]]), in_=pbk[0][:], func=AF.Copy), [pbk[1]], uXS)
        S.op(dve, I("tensor_copy", out=RF(XS_o + 2048, [[1, 16]]), in_=HALO[:]), [uHALO], uXS)

        if MODE == "host":
            if stop == l:
                dma(xch_out_d, RF(XS_o, [[1, XW]]), uXS, [])
                completed = False
                break
            gsrc = gin_d[l]
            ug = uXCH[l]
        else:
            dma(xch_loc_d[l], RF(XS_o, [[1, XW]]), uXS, [uXCH[l]])
            S.op(pool, I("collective_compute", "AllGather", ALU.bypass, replica_groups=[list(range(NCORE))],
                                                          ins=[xch_loc_d[l]], outs=[xch_all_d[l]]), [uXCH[l]], [uXCH[l]], dma=True, inc=1)
            gsrc = xch_all_d[l].rearrange("(r p) w -> r p w", p=128)
            ug = uXCH[l]

        dma(AP(HG, 0, [[NCORE * 16, 128], [16, NCORE], [1, 16]]), gsrc[:, :, 2048:2064].rearrange("r p w -> p r w"), [ug] + uXS, [uHG])
        S.op(dve, I("tensor_tensor", out=AP(SML, 0, [[64, 128], [8, 8], [1, 8]]), in0=AP(HG, 8, [[128, 128], [1, 8], [16, 8]]),
                                            in1=AP(PCA, A_SELP, [[NPCA, 128], [0, 8], [1, 8]]), op=ALU.mult), [uHG, uPCA, uSML], [uSML])
        S.op(dve, I("tensor_reduce", out=HALO[:, 0:8], in_=AP(SML, 0, [[64, 128], [8, 8], [1, 8]]), axis=AX.X, op=ALU.add), [uSML, uHALO], [uHALO])
        S.op(dve, I("tensor_tensor", out=AP(SML, 0, [[64, 128], [8, 8], [1, 8]]), in0=AP(HG, 0, [[128, 128], [1, 8], [16, 8]]),
                                            in1=AP(PCA, A_SELN, [[NPCA, 128], [0, 8], [1, 8]]), op=ALU.mult), [uHG, uPCA, uSML], [uSML])
        S.op(dve, I("tensor_reduce", out=HALO[:, 8:16], in_=AP(SML, 0, [[64, 128], [8, 8], [1, 8]]), axis=AX.X, op=ALU.add), [uSML, uHALO], [uHALO])

        (so_, us_) = bS
        (t1o, ut1) = bT1
        (t2o, ut2) = bT2

        def combine(d_):
            sv = RF(so_, [[128, 8], [1, 128]])
            S.op(dve, I("tensor_tensor", out=sv, in0=sv, in1=AP(COEF, d_ * 72 + 64, [[144, 128], [1, 8], [0, 128]]), op=ALU.mult), [us_, uCOEF], [us_])
            for i in range(NCORE):
                (ao, ua) = bT2
                dma(RF(ao, [[1, 1024]]), gsrc[i][:, d_ * 1024:(d_ + 1) * 1024], [ug], [ua])
                S.op(pool, I("tensor_tensor", out=RF(t1o, [[128, 8], [1, 128]]), in0=RF(ao, [[128, 8], [1, 128]]),
                                                                   in1=AP(COEF, d_ * 72 + i * 8, [[144, 128], [1, 8], [0, 128]]), op=ALU.mult), [ua, uCOEF], [ut1])
                S.op(dve, I("tensor_tensor", out=RF(so_, [[1, 1024]]), in0=RF(so_, [[1, 1024]]), in1=RF(t1o, [[1, 1024]]), op=ALU.add), [us_, ut1], [us_])

        def load_kv(tt, slot):
            (ko, uk) = bKTOK[slot]
            (vo, uv) = bV[slot]
            dma(RB(ko, [[1, 1024]]), KTOK_d[tt], [uKTOK[tt]], [uk])
            dma(RB(vo, [[1, 1024]]), V_d[tt], [uV[tt]], [uv])
            return ko, uk, vo, uv

        def kv_update(d_, ko, uk, vo, uv):
            (kso, uks) = bKS
            S.op(dve, I("tensor_tensor", out=RB(kso, [[128, 8], [1, 128]]), in0=RB(ko, [[128, 8], [1, 128]]),
                                                                in1=AP(KDEC, 8 * d_, [[16, 128], [1, 8], [0, 128]]), op=ALU.mult), [uk, uKDEC], [uks])
            for h in range(H):
                pbk = PB[6 + h // 4]
                S.mm(I("matmul", pbk[0][:, (h % 4) * 128:(h % 4) * 128 + 128], RB(kso + h * 128, [[1, 128]]), RB(vo + h * 128, [[1, 128]]), start=True, stop=True),
                     [uks, uv], [pbk[1]], inc=(h % 4 == 3))
            sv = RF(so_, [[128, 8], [1, 128]])
            S.op(pool, I("tensor_tensor", out=sv, in0=sv, in1=AP(CDEC, 8 * d_, [[16, 128], [1, 8], [0, 128]]), op=ALU.mult), [us_, uCDEC], [us_])
            for hb in range(2):
                S.op(dve, I("tensor_tensor", out=RF(so_ + hb * 512, [[1, 512]]), in0=RF(so_ + hb * 512, [[1, 512]]), in1=PB[6 + hb][0][:], op=ALU.add),
                     [us_, PB[6 + hb][1]], [us_])

        S.op(pool, I("memset", RF(so_, [[1, 1024]]), 0.0), [], [us_])
        border = [9, 8] + list(range(7, -1, -1))
        kbufs = (bKS, bPM)
        pbsets = (6, 4)

        def kv_pre(d_, tt, par):
            ko, uk, vo, uv = load_kv(tt, par)
            (kso, uks) = kbufs[par]
            S.op(dve, I("tensor_tensor", out=RB(kso, [[128, 8], [1, 128]]), in0=RB(ko, [[128, 8], [1, 128]]),
                        in1=AP(KDEC, 8 * d_, [[16, 128], [1, 8], [0, 128]]), op=ALU.mult), [uk, uKDEC], [uks])
            for h in range(H):
                pbk = PB[pbsets[par] + h // 4]
                S.mm(I("matmul", pbk[0][:, (h % 4) * 128:(h % 4) * 128 + 128], RB(kso + h * 128, [[1, 128]]), RB(vo + h * 128, [[1, 128]]), start=True, stop=True),
                     [uks, uv], [pbk[1]], inc=(h % 4 == 3))

        def s_upd(d_, par):
            sv = RF(so_, [[128, 8], [1, 128]])
            S.op(pool, I("tensor_tensor", out=sv, in0=sv, in1=AP(CDEC, 8 * d_, [[16, 128], [1, 8], [0, 128]]), op=ALU.mult), [us_, uCDEC], [us_])
            for hb in range(2):
                pbk = PB[pbsets[par] + hb]
                S.op(dve, I("tensor_tensor", out=RF(so_ + hb * 512, [[1, 512]]), in0=RF(so_ + hb * 512, [[1, 512]]), in1=pbk[0][:], op=ALU.add),
                     [us_, pbk[1]], [us_])

        kv_pre(1, border[0], 0)
        for n_, tt in enumerate(border):
            if tt == 7:
                combine(1)
            (so, uso) = bSBS[n_ % 2]
            S.op(act, I("activation", out=RB(so, [[1, 1024]]), in_=RF(so_, [[1, 1024]]), func=AF.Copy), [us_], [uso])
            dma(SBS_d[tt], RB(so, [[1, 1024]]), [uso], [uSBSd[tt]])
            if n_ + 1 < len(border) and border[n_ + 1] != 0:
                kv_pre(1, border[n_ + 1], (n_ + 1) % 2)
            if tt != 0:
                s_upd(1, n_ % 2)
            mod_emit(1)
        S.op(pool, I("memset", RF(so_, [[1, 1024]]), 0.0), [], [us_])
        forder = [8, 9] + list(range(8))
        def fwd_A(n_, tt):
            slot = n_ % 2
            if tt == 0:
                combine(0)
            (sfb, usfb) = bSFB
            S.op(act, I("activation", out=RB(sfb, [[1, 1024]]), in_=RF(so_, [[1, 1024]]), func=AF.Copy), [us_], [usfb])
            ko, uk, vo, uv = load_kv(tt, slot)
            (qo, uq) = bQ[slot]
            (kto, ukt) = bK[slot]
            (sbs, usbs) = bSBS[slot]
            tsl = slice(tt * 128, (tt + 1) * 128)
            dma(RB(qo, [[128, 8], [1, 128]]), QT_d[:, :, tsl].rearrange("h d t -> d h t"), uQT, [uq])
            dma(RB(kto, [[128, 8], [1, 128]]), KT_d[:, :, tsl].rearrange("h d t -> d h t"), uKT, [ukt])
            dma(RB(sbs, [[1, 1024]]), SBS_d[tt], [uSBSd[tt]], [usbs])
            for h in range(H):
                pbk = PB[h // 4]
                S.mm(I("matmul", pbk[0][:, (h % 4) * 128:(h % 4) * 128 + 128], RB(kto + h * 128, [[1, 128]]), RB(qo + h * 128, [[1, 128]]), start=True, stop=True),
                     [ukt, uq], [pbk[1]], inc=(h % 4 == 3))
            (pmo, upm) = bPM
            for hb in range(2):
                S.op(dve, I("tensor_tensor", out=RB(pmo + hb * 512, [[1, 512]]), in0=PB[hb][0][:], in1=MTB[:, hb * 512:(hb + 1) * 512], op=ALU.mult),
                     [PB[hb][1], uMTB], [upm])
            (qfo, uqf) = bQF
            (qbo, uqb) = bQB
            S.op(pool, I("tensor_tensor", out=RB(qfo, [[1, 1024]]), in0=RB(qo, [[1, 1024]]), in1=QDF[:], op=ALU.mult), [uq, uQDF], [uqf])
            S.op(pool, I("tensor_tensor", out=RB(qbo, [[1, 1024]]), in0=RB(qo, [[1, 1024]]), in1=QDB[:], op=ALU.mult), [uq, uQDB], [uqb])
            for h in range(H):
                pbk = PB[2 + h // 4]
                osl = slice((h % 4) * 128, (h % 4) * 128 + 128)
                S.mm(I("matmul", pbk[0][:, osl], RB(vo + h * 128, [[1, 128]]), RB(pmo + h * 128, [[1, 128]]), start=True, stop=False),
                     [uv, upm], [pbk[1]], inc=False)
                S.mm(I("matmul", pbk[0][:, osl], RB(sfb + h * 128, [[1, 128]]), RB(qfo + h * 128, [[1, 128]]), start=False, stop=False),
                     [usfb, uqf], [pbk[1]], inc=False)
                S.mm(I("matmul", pbk[0][:, osl], RB(sbs + h * 128, [[1, 128]]), RB(qbo + h * 128, [[1, 128]]), start=False, stop=True),
                     [usbs, uqb], [pbk[1]], inc=True)
            for hb in range(2):
                S.op(act, I("activation", out=GN1[:, hb * 512:(hb + 1) * 512], in_=PB[2 + hb][0][:], func=AF.Copy), [PB[2 + hb][1]], [uGN1])
            if tt != 7:
                kv_update(0, ko, uk, vo, uv)

        def fwd_B1(tt):
            for hb in range(2):
                S.mm(I("matmul", PB[4 + hb][0][:], ONESM[:], GN1[:, hb * 512:(hb + 1) * 512], start=True, stop=True), [uONESM, uGN1], [PB[4 + hb][1]], inc=True)
                S.op(dve, I("tensor_tensor", out=RF(t2o + hb * 512, [[1, 512]]), in0=GN1[:, hb * 512:(hb + 1) * 512].bitcast(F32), in1=PB[4 + hb][0][:], op=ALU.subtract),
                     [uGN1, PB[4 + hb][1]], [ut2])

        def fwd_B2(tt):
            (gzo, ugz) = bGZ[0]
            tsl = slice(tt * 128, (tt + 1) * 128)
            dma(RB(gzo, [[128, 8], [1, 128]]), UF_d[4][:, :, tsl].rearrange("h d t -> d h t"), uUF[4], [ugz])
            for hb in range(2):
                S.op(act, I("activation", out=GN3[:, hb * 512:(hb + 1) * 512], in_=RF(t2o + hb * 512, [[1, 512]]), func=AF.Square), [ut2], [uGN3])
                S.mm(I("matmul", PB[4 + hb][0][:], ONESM[:], GN3[:, hb * 512:(hb + 1) * 512], start=True, stop=True), [uONESM, uGN3], [PB[4 + hb][1]], inc=True)
            for hb in range(2):
                S.op(act, I("activation", out=RF(t1o + hb * 512, [[1, 512]]), in_=PB[4 + hb][0][:], func=AF.Ln, bias=EPSC[:, 0:1], scale=1.0), [PB[4 + hb][1], uEPSC], [ut1])
            S.op(act, I("activation", out=RF(t1o, [[1, 1024]]), in_=RF(t1o, [[1, 1024]]), func=AF.Exp, scale=-0.5), [ut1], [ut1])
            S.op(pool, I("tensor_tensor", out=RF(t2o, [[1, 1024]]), in0=RF(t2o, [[1, 1024]]), in1=RF(t1o, [[1, 1024]]), op=ALU.mult), [ut1, ut2], [ut2])
            S.op(pool, I("tensor_tensor", out=RF(t2o, [[128, 8], [1, 128]]), in0=RF(t2o, [[128, 8], [1, 128]]),
                         in1=AP(CST, cb + C_RNW, [[NCST, 128], [1, 8], [0, 128]]), op=ALU.mult), [ut2, uCST], [ut2])
            for h in range(H):
                S.op(dve if h % 2 == 0 else pool, I("tensor_tensor", out=YTv(h, tt * 128, tt * 128 + 128), in0=RF(t2o + h * 128, [[1, 128]]), in1=RB(gzo + h * 128, [[1, 128]]), op=ALU.mult),
                     [ut2, ugz], [YT[h][1]])

        pend_b2 = None
        for n_, tt in enumerate(forder):
            if l == L - 1 and tt >= 8:
                ko, uk, vo, uv = load_kv(tt, n_ % 2)
                kv_update(0, ko, uk, vo, uv)
                continue
            if tt == 0 and pend_b2 is not None:
                fwd_B2(pend_b2)
                pend_b2 = None
            fwd_A(n_, tt)
            if pend_b2 is not None:
                fwd_B2(pend_b2)
            fwd_B1(tt)
            pend_b2 = tt
            mod_emit(2)
        fwd_B2(pend_b2)

        def out_proj(part):
            for b in range(8):
                wt, uwt = w_next()
                for half in range(2):
                    dmc = b * 2 + half
                    for ri, (t0, t1, j) in enumerate(RANGES):
                        if l == L - 1 and j == 1:
                            continue
                        n = t1 - t0
                        pb, upb = next_pb(0, 5)
                        for k in range(8):
                            S.mm(I("matmul", pb[:, 0:n], wt[:, k * 256 + half * 128:k * 256 + half * 128 + 128], YTv(k, t0, t1), start=(k == 0), stop=(k == 7)),
                                 [uwt, YT[k][1]], [upb], inc=(k == 7))
                        S.op(dve, I("scalar_tensor_tensor", out=XT[dmc][0][:, t0:t1], in0=pb[:, 0:n], scalar=MVEC[:, mvec(j, 2) + dmc:mvec(j, 2) + dmc + 1], in1=XT[dmc][0][:, t0:t1], op0=ALU.mult, op1=ALU.add),
                             [upb, uMVEC, XT[dmc][1]], [XT[dmc][1]])
                    if part == 0:
                        if dmc >= 3:
                            rms_k(dmc - 3)
            if part == 0:
                for k_ in range(KC - 3, KC):
                    rms_k(k_)

        mod_emit(24)
        out_proj(1)

        old_all = list(reg["units"])
        keep = [u for (_, u) in YT]
        reg["off"] = YT[7][0] * 2 + ((TT * 2 + 63) // 64) * 64
        reg["units"] = list(keep)
        cin = [[r_b16(TT, "cin%d_%d" % (g, i)) for i in range(2 if g < 2 else 1)] for g in range(4)]
        CH_o, uCH = r_f32(TT + 8, "CH")
        CV_o, uCV = r_f32(TT + 8, "CV")
        Yb = [r_f32(TT, "Y%d" % i) for i in range(1)]
        SZ_o, uSZ = r_f32(TT, "SZ")
        SSC2_o, uSSC2 = r_f32(TT, "SSC2")
        barrier(old_all, reg["units"])
        S.op(pool, I("memset", RF(CH_o, [[1, TT + 8]]), 0.0), [], [uCH])
        cw = cb + C_CONVW
        for cc in range(8):
            ins_ = []
            for g in range(4):
                (o, u) = cin[g][cc % len(cin[g])]
                dma(RB(o, [[1, TT]]), UF_d[g, cc], [uUF[g][cc]], [u])
                ins_.append((o, u))
            (ho, uh), (co, uc), (bo, ub), (zo, uz) = ins_
            S.op(pool, I("tensor_copy", out=RF(CH_o, [[1, 1]]), in_=HALO[:, cc:cc + 1]), [uHALO, uCH], [uCH])
            S.op(pool, I("tensor_copy", out=RF(CH_o + 1025, [[1, 1]]), in_=HALO[:, 8 + cc:9 + cc]), [uHALO, uCH], [uCH])
            S.op(dve, I("tensor_tensor", out=RF(CH_o + 1, [[1, T]]), in0=RB(ho, [[1, T]]), in1=RB(co, [[1, T]]), op=ALU.mult), [uh, uc, uCH], [uCH])
            S.op(dve, I("tensor_tensor", out=RF(CH_o + 1027, [[1, TC]]), in0=RB(ho + T, [[1, TC]]), in1=RB(co + T, [[1, TC]]), op=ALU.mult), [uh, uc, uCH], [uCH])
            for (c0, n, o0) in ((1, T, 0), (1027, TC, T)):
                S.op(act, I("activation", out=RF(CV_o + o0, [[1, n]]), in_=RF(CH_o + c0 - 1, [[1, n]]), func=AF.Identity, scale=CST[:, cw + cc:cw + cc + 1]), [uCH, uCST, uCV], [uCV])
                S.op(dve, I("scalar_tensor_tensor", out=RF(CV_o + o0, [[1, n]]), in0=RF(CH_o + c0, [[1, n]]), scalar=CST[:, cw + 8 + cc:cw + 9 + cc], in1=RF(CV_o + o0, [[1, n]]), op0=ALU.mult, op1=ALU.add), [uCH, uCST, uCV], [uCV])
                S.op(dve, I("scalar_tensor_tensor", out=RF(CV_o + o0, [[1, n]]), in0=RF(CH_o + c0 + 1, [[1, n]]), scalar=CST[:, cw + 16 + cc:cw + 17 + cc], in1=RF(CV_o + o0, [[1, n]]), op0=ALU.mult, op1=ALU.add), [uCH, uCST, uCV], [uCV])
            (yo, uy) = Yb[0]
            S.op(dve, I("tensor_tensor", out=RF(yo, [[1, TT]]), in0=RF(CV_o, [[1, TT]]), in1=RB(bo, [[1, TT]]), op=ALU.mult), [uCV, ub], [uy])
            for ri, (t0, t1, _) in enumerate(RANGES):
                n = t1 - t0
                (sq, usq) = SQR[(cc * 3 + ri) % 2]
                S.op(act, I("activation", out=sq[:, 0:n], in_=RF(yo + t0, [[1, n]]), func=AF.Square), [uy], [usq])
                S.mm(I("matmul", PB[ri][0][:, 0:n], ONES[:], sq[:, 0:n], start=(cc == 0), stop=(cc == 7)), [uONES, usq], [PB[ri][1]], inc=True)
            S.op(act, I("activation", out=RF(SZ_o, [[1, TT]]), in_=RB(zo, [[1, TT]]), func=AF.Silu), [uz], [uSZ])
            S.op(dve, I("scalar_tensor_tensor", out=YTv(cc), in0=RF(yo, [[1, TT]]), scalar=CST[:, cb + C_CNW + cc:cb + C_CNW + cc + 1], in1=RF(SZ_o, [[1, TT]]), op0=ALU.mult, op1=ALU.mult), [uy, uSZ, uCST], [YT[cc][1]])
        for ri, (t0, t1, _) in enumerate(RANGES):
            n = t1 - t0
            S.op(act, I("activation", out=RSTD[:, t0:t1], in_=PB[ri][0][:, 0:n], func=AF.Ln, bias=EPSC[:, 0:1], scale=1.0 / 1024.0), [PB[ri][1], uEPSC], [uRSTD])
        S.op(act, I("activation", out=RSTD[:], in_=RSTD[:], func=AF.Exp, scale=-0.5), [uRSTD], [uRSTD])
        for cc in range(8):
            S.op(dve if cc % 2 == 0 else pool, I("tensor_tensor", out=YTv(cc), in0=YTv(cc), in1=RSTD[:], op=ALU.mult), [YT[cc][1], uRSTD], [YT[cc][1]])
        out_proj(0)
        if DBG and l == 0:
            dbg_d = nc.dram_tensor("dbgx", [D, TT], F32, kind="ExternalOutput").ap()
            for k in range(KC):
                dma(dbg_d[k * 128:(k + 1) * 128, :], XT[k][0][:], [XT[k][1]], [])

    if completed:
        old = r_reset()
        OST = [r_f32(512, "OST%d" % i) for i in range(2)]
        barrier(old, reg["units"])
        rms_finish()
        for k in range(KC):
            for ri, (t0, t1, j) in enumerate(RANGES[:2]):
                n = t1 - t0
                (oo, uo) = OST[(k * 2 + ri) % 2]
                S.op(dve, I("scalar_tensor_tensor", out=RF(oo, [[1, n]]), in0=XT[k][0][:, t0:t1], scalar=CST[:, C_FNW + k:C_FNW + k + 1], in1=RSTD[:, t0:t1], op0=ALU.mult, op1=ALU.mult),
                     [XT[k][1], uRSTD, uCST], [uo])
                dma(out_d[k * 128:(k + 1) * 128, t0:t1], RF(oo, [[1, n]]), [uo], [])
    S.final_wait_all()
    S.replay()
    return nc


def _pretile(w, cols_per_block, kc=KC):
    out = []
    for cols in cols_per_block:
        blk = w[:, cols]
        blk = blk.reshape(kc, 128, 256).transpose(1, 0, 2).reshape(128, kc * 256)
        out.append(blk)
    return np.ascontiguousarray(np.stack(out, 0))


def _fm(v):
    return np.ascontiguousarray(v.reshape(-1, 128).T)


def _host_prep(x, c, ctx, c_ctx, norm_w, w_mod, b_mod, w_in, conv_w, conv_norm_w, ret_norm_w,
               ret_decay_f, ret_decay_b, w_out, final_norm_w):
    f32 = np.float32
    x = np.asarray(x, f32); ctx = np.asarray(ctx, f32)
    cst = np.zeros((128, NCST), f32)
    for l in range(L):
        cb = l * CL
        cst[:, cb + C_NW:cb + C_NW + 16] = _fm(np.asarray(norm_w[l], f32))
        cst[:, cb + C_BMOD:cb + C_BMOD + 48] = _fm(np.asarray(b_mod[l], f32))
        for k in range(3):
            cst[:, cb + C_CONVW + 8 * k:cb + C_CONVW + 8 * k + 8] = _fm(np.asarray(conv_w[l][k], f32))
        cst[:, cb + C_CNW:cb + C_CNW + 8] = _fm(np.asarray(conv_norm_w[l], f32))
        cst[:, cb + C_RNW:cb + C_RNW + 8] = _fm(np.asarray(ret_norm_w[l], f32))
        cst[:, cb + C_DECF:cb + C_DECF + 8] = np.asarray(ret_decay_f[l], f32)[None, :]
        cst[:, cb + C_DECB:cb + C_DECB + 8] = np.asarray(ret_decay_b[l], f32)[None, :]
    cst[:, C_FNW:C_FNW + 16] = _fm(np.asarray(final_norm_w, f32))
    c2 = np.ascontiguousarray(np.stack([_fm(np.asarray(c, f32).reshape(-1)), _fm(np.asarray(c_ctx, f32).reshape(-1))], -1).reshape(128, KC * 2))
    b256 = lambda n: [np.arange(b * 256, (b + 1) * 256) for b in range(n)]
    wmod = np.stack([_pretile(np.asarray(w_mod[l], f32), b256(24)) for l in range(L)], 0)
    win = np.stack([_pretile(np.asarray(w_in[l], f32), [_block_cols(k, i) for (k, i) in BLOCKS]) for l in range(L)], 0)
    wout = np.stack([np.stack([_pretile(np.asarray(w_out[l], f32)[part * 1024:(part + 1) * 1024], b256(8), kc=8) for part in range(2)], 0) for l in range(L)], 0)
    ctxT = ctx[0].T
    j = np.arange(128, dtype=np.float64)
    i = np.arange(128, dtype=np.float64)
    inv = (10000.0 ** (-np.arange(32, dtype=np.float32) / np.float32(32.0))).astype(np.float32)
    ks = 128.0 ** -0.5
    maps = []
    for r in range(NCORE):
        pca = np.zeros((128, NPCA), np.float64)
        pcb = np.zeros((128, NPCB), np.float64)
        for tt in range(8):
            pca[:, A_EAF + tt] = 1023 - 128 * tt - j
            pca[:, A_EAB + tt] = 128 * tt + j
        pca[:, A_KDF] = 127 - j
        pca[:, A_KDB] = j
        pca[:, A_QDF:A_QDF + 128] = (i + 1)[None, :]
        pca[:, A_QDB:A_QDB + 128] = (128 - i)[None, :]
        dm = i[None, :] - j[:, None]
        pcb[:, B_MRF:B_MRF + 128] = np.maximum(dm, 0)
        pcb[:, B_MRB:B_MRB + 128] = np.maximum(-dm, 0)
        pcb[:, B_MU:B_MU + 128] = ks * (dm > 0)
        pcb[:, B_ML:B_ML + 128] = ks * (dm < 0)
        pcb[:, B_MI:B_MI + 128] = 2.0 * ks * (dm == 0)
        pcb[:, B_ID:B_ID + 128] = np.eye(128)
        for s in range(NCORE):
            pca[:, A_EXC + s] = 1024 * (r - 1 - s) if s < r else 0
            pca[:, A_MSK + s] = 1.0 if s < r else 0.0
            pca[:, A_EXC + 9 + s] = 1024 * (s - r - 1) if s > r else 0
            pca[:, A_MSK + 9 + s] = 1.0 if s > r else 0.0
        pca[:, A_EXC + 8] = 1024 * r
        pca[:, A_MSK + 8] = 1.0
        pca[:, A_EXC + 17] = 1024 * (NCORE - 1 - r)
        pca[:, A_MSK + 17] = 1.0
        if r > 0:
            pca[:, A_SELP + r - 1] = 1.0
        if r < NCORE - 1:
            pca[:, A_SELN + r + 1] = 1.0
        for tt in range(NT):
            base = B_ROPE + tt * 192
            if tt < 8:
                tg = r * T + tt * 128 + np.arange(128)
                rowp = (tg // 64).astype(np.float32)
                colp = (tg % 64).astype(np.float32)
                ang = np.concatenate([rowp[:, None] * inv[None, :], colp[:, None] * inv[None, :]], 1).astype(np.float64)
                cs, sn = np.cos(ang), np.sin(ang)
            else:
                cs, sn = np.ones((128, 64)), np.zeros((128, 64))
            pcb[:, base:base + 64] = cs
            ssn = np.stack([-sn, sn], 2)
            pcb[:, base + 64:base + 192] = ssn.reshape(128, 128)
        xT = np.ascontiguousarray(np.concatenate([x[0, r * T:(r + 1) * T, :].T, ctxT], 1))
        maps.append({"xT": xT, "c2": c2, "cst": cst, "pca": pca.astype(f32), "pcb": pcb.astype(f32),
                     "wmod": wmod, "win": win, "wout": wout})
    return maps


_PROGS = {}


def _prog(stop):
    if stop not in _PROGS:
        _PROGS[stop] = build_program(stop)
    return _PROGS[stop]


def kernel(**inputs):
    maps = _host_prep(**inputs)
    cores = list(range(NCORE))
    if MODE == "host":
        gins = []
        for e in range(L):
            m = [dict(mm_, **{"gin%d" % i: g for i, g in enumerate(gins)}) for mm_ in maps]
            res = run_bass_kernel_spmd(_prog(e), m, core_ids=cores)
            gins.append(np.ascontiguousarray(np.stack([np.asarray(r["xch"], np.float32) for r in res.results], 0)))
        m = [dict(mm_, **{"gin%d" % i: g for i, g in enumerate(gins)}) for mm_ in maps]
        res = run_bass_kernel_spmd(_prog(None), m, core_ids=cores)
    else:
        res = run_bass_kernel_spmd(_prog(None), maps, core_ids=cores)
    out = np.concatenate([np.asarray(r["outT"], np.float32).T for r in res.results], 0)
    return out[None].astype(np.float32)
```

```python
import numpy as np
import concourse.bass as bass
import concourse.mybir as mybir
from concourse.bass_utils import run_bass_kernel_spmd

F32 = mybir.dt.float32
F32R = mybir.dt.float32r
BF16 = mybir.dt.bfloat16
ALU = mybir.AluOpType
AF = mybir.ActivationFunctionType
AX = mybir.AxisListType

NCORE = 8
L = 2
D = 2048
KC = 16
T = 1024
TC = 256
TT = T + TC
NT = TT // 128
H = 8
EPS = 1e-6
RANGES = ((0, 512, 0), (512, 1024, 0), (1024, 1280, 1))
XW = 2064

DBG = False
MODE = "host"

ENGS = ("pe", "act", "dve", "pool", "sp")


class Unit:
    __slots__ = ("name", "last_w", "readers")

    def __init__(self, name):
        self.name = name
        self.last_w = None
        self.readers = {}


class Sched:
    NDMA = 48
    NDMA_SP = 40

    def __init__(self, nc):
        self.nc = nc
        self.streams = {e: [] for e in ENGS}
        self.esem = {e: nc.alloc_semaphore("s_" + e) for e in ENGS if e != "sp"}
        self.ecnt = {e: 0 for e in ENGS}
        self.dsem = [nc.alloc_semaphore("d%d" % i) for i in range(self.NDMA)]
        self.dcnt = [0] * self.NDMA
        self.dnext = 0
        self.pnext = 0
        self.waited = {e: {} for e in ENGS}

    def _sem(self, key):
        return self.esem[key[1]] if key[0] == "e" else self.dsem[key[1]]

    def _need(self, eng, tok, need):
        if tok is None:
            return
        key, val, src = tok
        if src == "pe" and eng == "pe":
            return
        if need.get(key, 0) < val:
            need[key] = val

    def _emit_waits(self, eng, need):
        w = self.waited[eng]
        for key, val in need.items():
            if w.get(key, 0) >= val:
                continue
            w[key] = val
            self.streams[eng].append(("wait", self._sem(key), val))

    @staticmethod
    def _addr(u, tok):
        o = u.readers.get(tok[0])
        if o is None or o[1] < tok[1]:
            u.readers[tok[0]] = tok

    def _deps(self, eng, reads, writes):
        need = {}
        for u in reads:
            self._need(eng, u.last_w, need)
        for u in writes:
            self._need(eng, u.last_w, need)
            for t in u.readers.values():
                self._need(eng, t, need)
        return need

    def _commit(self, tok, reads, writes):
        for u in reads:
            self._addr(u, tok)
        for u in writes:
            u.last_w = tok
            u.readers = {}

    def op(self, eng, fn, reads=(), writes=(), dma=None, inc=16):
        if dma is None:
            dma = (eng == "sp")
        need = self._deps(eng, reads, writes)
        if dma:
            if eng == "pool":
                idx = self.NDMA_SP + self.pnext
                self.pnext = (self.pnext + 1) % (self.NDMA - self.NDMA_SP)
            else:
                idx = self.dnext
                self.dnext = (self.dnext + 1) % self.NDMA_SP
            key = ("d", idx)
            if self.dcnt[idx] > 0 and need.get(key, 0) < self.dcnt[idx]:
                need[key] = self.dcnt[idx]
            self._emit_waits(eng, need)
            self.dcnt[idx] += inc
            tok = (key, self.dcnt[idx], "dma")
            self.streams[eng].append(("ins", fn, self.dsem[idx], inc))
        else:
            self._emit_waits(eng, need)
            self.ecnt[eng] += 1
            tok = (("e", eng), self.ecnt[eng], eng)
            self.streams[eng].append(("ins", fn, self.esem[eng], 1))
        self._commit(tok, reads, writes)
        return tok

    def mm(self, fn, reads=(), writes=(), inc=True):
        if inc:
            return self.op("pe", fn, reads, writes)
        need = self._deps("pe", reads, writes)
        self._emit_waits("pe", need)
        self.streams["pe"].append(("ins0", fn))
        tok = (("e", "pe"), self.ecnt["pe"] + 1, "pe")
        self._commit(tok, reads, writes)
        return tok

    def final_wait_all(self):
        need = {}
        for i in range(self.NDMA):
            if self.dcnt[i] > 0:
                need[("d", i)] = self.dcnt[i]
        for e in self.esem:
            if self.ecnt[e] > 0:
                need[("e", e)] = self.ecnt[e]
        self._emit_waits("sp", need)

    def replay(self):
        nc = self.nc
        with nc.Block() as block:
            def run(stream):
                def f(eng):
                    for it in stream:
                        if it[0] == "wait":
                            eng.wait_ge(it[1], it[2])
                        elif it[0] == "ins":
                            it[1](eng).then_inc(it[2], it[3])
                        else:
                            it[1](eng)
                return f
            block.tensor(run(self.streams["pe"]))
            block.scalar(run(self.streams["act"]))
            block.vector(run(self.streams["dve"]))
            block.gpsimd(run(self.streams["pool"]))
            block.sync(run(self.streams["sp"]))


def I(method, *a, **kw):
    return lambda e: getattr(e, method)(*a, **kw)


def AP(t, off, dims):
    return bass.AP(t, off, [list(d) for d in dims])


C_NW, C_BMOD, C_CONVW, C_CNW, C_RNW, C_DECF, C_DECB = 0, 16, 64, 88, 96, 104, 112
CL = 120
C_FNW = L * CL
NCST = L * CL + 16
A_EAF, A_EAB, A_KDF, A_KDB = 0, 8, 16, 17
A_EXC, A_MSK = 18, 36
A_SELP, A_SELN = 54, 62
A_QDF, A_QDB = 70, 198
NPCA = 326
B_MRF, B_MRB, B_MU, B_ML, B_MI = 0, 128, 256, 384, 512
B_ID = 640
B_ROPE = 768
NPCB = B_ROPE + NT * 192

N_INBLK = 32
BLOCKS = ([("k", b) for b in range(4)] + [("v", b) for b in range(4)] +
          [x for cc in range(8) for x in (("hc", cc), ("bz", cc))] +
          [("q", b) for b in range(4)] + [("zr", b) for b in range(4)])


def _block_cols(kind, i):
    r128 = np.arange(128)
    r256 = np.arange(256)
    if kind in ("k", "q"):
        base = (5 if kind == "k" else 4) * 1024 + i * 256
        hf, ii, hh, ss = np.meshgrid(np.arange(2), np.arange(32), np.arange(2), np.arange(2), indexing="ij")
        return (base + hh * 128 + hf * 64 + ss * 32 + ii).reshape(-1)
    if kind == "v":
        return 6 * 1024 + i * 256 + r256
    if kind == "zr":
        return 7 * 1024 + i * 256 + r256
    if kind == "hc":
        return np.concatenate([0 * 1024 + i * 128 + r128, 2 * 1024 + i * 128 + r128])
    if kind == "bz":
        return np.concatenate([1 * 1024 + i * 128 + r128, 3 * 1024 + i * 128 + r128])
    raise ValueError(kind)


def build_program(stop):
    nc = bass.Bass("TRN2", target_bir_lowering=False)
    S = Sched(nc)
    sp, act, dve, pool = "sp", "act", "dve", "pool"

    def dram_in(name, shape, dt=F32):
        return nc.dram_tensor(name, list(shape), dt, kind="ExternalInput").ap()

    xT_d = dram_in("xT", [D, TT])
    c2_d = dram_in("c2", [128, KC * 2])
    cst_d = dram_in("cst", [128, NCST])
    pca_d = dram_in("pca", [128, NPCA])
    pcb_d = dram_in("pcb", [128, NPCB])
    wmod_d = dram_in("wmod", [L, 24, 128, KC * 256])
    win_d = dram_in("win", [L, N_INBLK, 128, KC * 256])
    wout_d = dram_in("wout", [L, 2, 8, 128, 8 * 256])
    n_ex = L if stop is None else stop
    gin_d = [dram_in("gin%d" % e, [NCORE, 128, XW]) for e in range(n_ex)] if MODE == "host" else []
    if stop is None:
        out_d = nc.dram_tensor("outT", [D, T], F32, kind="ExternalOutput").ap()
    else:
        xch_out_d = nc.dram_tensor("xch", [128, XW], F32, kind="ExternalOutput").ap()

    def scratch(name, shape, dt=BF16):
        return nc.dram_tensor(name, list(shape), dt).ap()
    QT_d = scratch("QTs", [H, 128, TT]); uQT = [Unit("QT%d" % h) for h in range(H)]
    KT_d = scratch("KTs", [H, 128, TT]); uKT = [Unit("KT%d" % h) for h in range(H)]
    KTOK_d = scratch("KTOKs", [NT, 128, 1024]); uKTOK = [Unit("KTOK%d" % t) for t in range(NT)]
    V_d = scratch("Vs", [NT, 128, 1024]); uV = [Unit("V%d" % t) for t in range(NT)]
    UF_d = scratch("UFs", [5, 8, 128, TT]); uUF = [[Unit("UF%d_%d" % (g, c)) for c in range(8)] for g in range(5)]
    SBS_d = scratch("SBSs", [NT, 128, 1024]); uSBSd = [Unit("SBSd%d" % t) for t in range(NT)]
    if MODE == "cc":
        xch_loc_d = [scratch("xchl%d" % e, [128, XW], F32) for e in range(L)]
        xch_all_d = [scratch("xcha%d" % e, [NCORE * 128, XW], F32) for e in range(L)]
    uXCH = [Unit("xch%d" % e) for e in range(L)]

    def sb(name, shape, dt=F32):
        return nc.alloc_sbuf_tensor(name, list(shape), dt), Unit(name)

    XT = [sb("XT%d" % k, [128, TT]) for k in range(KC)]
    NWB = 3
    WB = [sb("WB%d" % i, [128, KC * 256], BF16) for i in range(NWB)]
    CST, uCST = sb("CST", [128, NCST])
    PCA, uPCA = sb("PCA", [128, NPCA])
    SC, uSC = sb("SC", [128, KC * 2], BF16)
    SCF, uSCF = sb("SCF", [128, KC * 2])
    ONES, uONES = sb("ONES", [128, 128], F32R)
    ONEF, uONEF = sb("ONEF", [128, 128])
    ONESM, uONESM = sb("ONESM", [128, 128], F32R)
    IDB, uIDB = sb("IDB", [128, 128], BF16)
    I2, uI2 = sb("I2", [2, 2])
    EPSC, uEPSC = sb("EPSC", [128, 1])
    MODTS = [sb("MODT%d" % l_, [128, 96]) for l_ in range(L)]
    MVECS = [sb("MVEC%d" % l_, [128, 6 * 16]) for l_ in range(L)]
    LG, uLG = sb("LG", [128, 16])
    RSTD, uRSTD = sb("RSTD", [128, TT])
    MR, uMR = sb("MR", [2, 256])
    COEF, uCOEF = sb("COEF", [128, 2 * 9 * 8])
    DAF, uDAF = sb("DAF", [128, 64])
    DAB, uDAB = sb("DAB", [128, 64])
    KDEC, uKDEC = sb("KDEC", [128, 16])
    CDEC, uCDEC = sb("CDEC", [128, 16])
    QDF, uQDF = sb("QDF", [128, 1024])
    QDB, uQDB = sb("QDB", [128, 1024])
    MTB, uMTB = sb("MTB", [128, 1024])
    HALO, uHALO = sb("HALO", [128, 16])
    HBND, uHBND = sb("HBND", [128, 16])
    HG, uHG = sb("HG", [128, NCORE * 16])
    SML, uSML = sb("SML", [128, 64])
    BAR, uBAR = sb("BAR", [128, 1])
    LNKS, uLNKS = sb("LNKS", [128, 1])
    SQR = [sb("SQR%d" % i, [128, 512], F32R) for i in range(2)]
    GN1, uGN1 = sb("GN1", [128, 1024], F32R)
    GN3, uGN3 = sb("GN3", [128, 1024], F32R)

    HW = (nc.sbuf_bytes_remaining - 256) // 4 // 16 * 16
    HREG = nc.alloc_sbuf_tensor("HREG", [128, HW], F32)
    v_f32 = HREG[:]
    v_b16 = HREG[:].bitcast(BF16)
    v_f32r = HREG[:].bitcast(F32R)

    def RF(off, dims):
        return bass.AP(v_f32.tensor, off, [[v_f32.ap[0][0], 128]] + [list(d) for d in dims])

    def RB(off, dims):
        return bass.AP(v_b16.tensor, off, [[v_b16.ap[0][0], 128]] + [list(d) for d in dims])

    def RR(off, dims):
        return bass.AP(v_f32r.tensor, off, [[v_f32r.ap[0][0], 128]] + [list(d) for d in dims])

    reg = {"off": 0, "units": []}

    def r_reset():
        reg["off"] = 0
        old = reg["units"]
        reg["units"] = []
        return old

    def r_alloc(nbytes, name):
        o = reg["off"]
        reg["off"] += ((nbytes + 63) // 64) * 64
        assert reg["off"] <= HW * 4, ("phase region overflow", name, reg["off"], HW * 4)
        u = Unit(name)
        reg["units"].append(u)
        return o, u

    def r_b16(n, name):
        o, u = r_alloc(n * 2, name)
        return o // 2, u

    def r_f32(n, name):
        o, u = r_alloc(n * 4, name)
        return o // 4, u

    def barrier(old_units, new_units):
        S.op(pool, I("memset", BAR[:], 0.0), [], list(old_units) + list(new_units) + [uBAR])

    PB = []
    for i in range(8):
        PB.append((nc.alloc_psum_tensor("PB%d" % i, [128, 512], F32), Unit("PB%d" % i)))

    def dma(out, in_, reads, writes):
        S.op(sp, I("dma_start", out=out, in_=in_), reads, writes)

    for k in range(KC):
        dma(XT[k][0][:], xT_d[k * 128:(k + 1) * 128, :], [], [XT[k][1]])
    dma(CST[:], cst_d, [], [uCST])
    dma(PCA[:], pca_d, [], [uPCA])
    dma(SCF[:], c2_d, [], [uSCF])
    dma(I2[:], pcb_d[0:2, B_ID:B_ID + 2], [], [uI2])
    S.op(act, I("activation", out=SC[:], in_=SCF[:], func=AF.Silu), [uSCF], [uSC])
    S.op(pool, I("memset", ONEF[:], 1.0), [], [uONEF])
    S.op(act, I("activation", out=ONES[:], in_=ONEF[:], func=AF.Identity), [uONEF], [uONES])
    S.op(act, I("activation", out=ONESM[:], in_=ONEF[:], func=AF.Identity, scale=1.0 / 128.0), [uONEF], [uONESM])
    S.op(pool, I("memset", EPSC[:], EPS), [], [uEPSC])
    S.op(pool, I("memset", LNKS[:], float(np.log(128.0 ** -0.5))), [], [uLNKS])

    wctr = {"n": 0}

    def wload(src, ncols):
        i = wctr["n"] % NWB
        wctr["n"] += 1
        wt, uwt = WB[i]
        S.op(pool, I("dma_start", out=wt[:, 0:ncols], in_=src), [], [uwt], dma=True)
        return WB[i]

    pbrot = {"n": 0}

    def next_pb(lo=0, hi=6):
        i = lo + pbrot["n"] % (hi - lo)
        pbrot["n"] += 1
        return PB[i]

    def mvec(j, which):
        return (which * 2 + j) * 16

    wsrcs = []
    wsrcs += [(wmod_d[0, b], KC * 256) for b in range(24)]
    SKIP_KINDS = ("bz", "q", "zr")
    for l in range(L):
        wsrcs += [(win_d[l, b], KC * 256) for b in range(N_INBLK) if not (stop == l and BLOCKS[b][0] in SKIP_KINDS)]
        if l + 1 < L:
            wsrcs += [(wmod_d[l + 1, b], KC * 256) for b in range(24)]
        wsrcs += [(wout_d[l, 1, b], 8 * 256) for b in range(8)]
        wsrcs += [(wout_d[l, 0, b], 8 * 256) for b in range(8)]
    wpos = {"n": 0}
    wq = []

    def w_next():
        while len(wq) < NWB - 1 and wpos["n"] < len(wsrcs):
            src, ncols = wsrcs[wpos["n"]]
            wpos["n"] += 1
            wq.append(wload(src, ncols))
        cur = wq.pop(0)
        while len(wq) < NWB - 1 and wpos["n"] < len(wsrcs):
            src, ncols = wsrcs[wpos["n"]]
            wpos["n"] += 1
            wq.append(wload(src, ncols))
        return cur

    RBANKS = (PB[5], PB[6], PB[7])

    def rms_k(k):
        for ri, (t0, t1, _) in enumerate(RANGES):
            (sq, usq) = SQR[(k * 3 + ri) % 2]
            n = t1 - t0
            S.op(act, I("activation", out=sq[:, 0:n], in_=XT[k][0][:, t0:t1], func=AF.Square),
                 [XT[k][1]], [usq])
            S.mm(I("matmul", RBANKS[ri][0][:, 0:n], ONES[:], sq[:, 0:n], start=(k == 0), stop=(k == KC - 1)),
                 [uONES, usq], [RBANKS[ri][1]], inc=True)

    def rms_finish():
        for ri, (t0, t1, _) in enumerate(RANGES):
            n = t1 - t0
            S.op(act, I("activation", out=RSTD[:, t0:t1], in_=RBANKS[ri][0][:, 0:n], func=AF.Ln, bias=EPSC[:, 0:1], scale=1.0 / D),
                 [RBANKS[ri][1], uEPSC], [uRSTD])
        S.op(act, I("activation", out=RSTD[:], in_=RSTD[:], func=AF.Exp, scale=-0.5), [uRSTD], [uRSTD])

    def mod_block(lm, b):
        MODT, uMODT = MODTS[lm]
        wt, uwt = w_next()
        pb, upb = PB[0]
        for k in range(KC):
            S.mm(I("matmul", pb[0:2, 0:256], SC[:, 2 * k:2 * k + 2], wt[:, k * 256:(k + 1) * 256], start=(k == 0), stop=(k == KC - 1)),
                 [uSC, uwt], [upb], inc=(k == KC - 1))
        S.op(act, I("activation", out=MR[:], in_=pb[0:2, 0:256], func=AF.Copy), [upb], [uMR])
        pb2, upb2 = PB[1]
        for hh in range(2):
            S.mm(I("matmul", pb2[:, 2 * hh:2 * hh + 2], MR[0:2, hh * 128:(hh + 1) * 128], I2[:], start=True, stop=True),
                 [uMR, uI2], [upb2], inc=True)
        S.op(dve, I("tensor_copy", out=MODT[:, 4 * b:4 * b + 4], in_=pb2[:, 0:4]), [upb2], [uMODT])

    def mod_finish(lm):
        MODT, uMODT = MODTS[lm]
        MVEC, uMVEC = MVECS[lm]
        cbm = lm * CL
        S.op(dve, I("tensor_tensor", out=AP(MODT, 0, [[96, 128], [1, 2], [2, 48]]), in0=AP(MODT, 0, [[96, 128], [1, 2], [2, 48]]),
                    in1=AP(CST, cbm + C_BMOD, [[NCST, 128], [0, 2], [1, 48]]), op=ALU.add),
             [uMODT, uCST], [uMODT])
        for j in range(2):
            shift = AP(MODT, j, [[96, 128], [2, 16]])
            scale = AP(MODT, 32 + j, [[96, 128], [2, 16]])
            gate = AP(MODT, 64 + j, [[96, 128], [2, 16]])
            S.op(dve, I("scalar_tensor_tensor", out=MVEC[:, mvec(j, 0):mvec(j, 0) + 16], in0=scale, scalar=1.0, in1=CST[:, cbm + C_NW:cbm + C_NW + 16], op0=ALU.add, op1=ALU.mult),
                 [uMODT, uCST], [uMVEC])
            S.op(dve, I("tensor_copy", out=MVEC[:, mvec(j, 1):mvec(j, 1) + 16], in_=shift), [uMODT], [uMVEC])
            S.op(dve, I("tensor_copy", out=MVEC[:, mvec(j, 2):mvec(j, 2) + 16], in_=gate), [uMODT], [uMVEC])

    modq = {"l": 0, "b": 0}

    def mod_emit(n):
        for _ in range(n):
            if modq["l"] >= L or modq["b"] >= 24:
                return
            mod_block(modq["l"], modq["b"])
            modq["b"] += 1
            if modq["b"] == 24:
                mod_finish(modq["l"])

    completed = True
    for l in range(L):
        cb = l * CL
        old = r_reset()
        HX = [r_b16(TT, "HX%d" % k) for k in range(KC)]
        PCB_o, uPCB = r_f32(NPCB, "PCB")
        STG = [r_b16(512, "STG%d" % i) for i in range(4)]
        STT = [r_b16(TT, "STT%d" % i) for i in range(2)]
        RTA_o, uRTA = r_f32(256, "RTA")
        RTB_o, uRTB = r_f32(256, "RTB")
        SQT = [r_f32(512, "TMPA%d" % i) for i in range(2)]
        barrier(old, reg["units"])
        dma(RF(PCB_o, [[1, NPCB]]), pcb_d, [], [uPCB])
        if l == 0:
            S.op(dve, I("tensor_copy", out=IDB[:], in_=RF(PCB_o + B_ID, [[1, 128]])), [uPCB], [uIDB])

        def PCBc(col, dims, PCB_o=PCB_o):
            return RF(PCB_o + col, dims)

        if l == 0:
            for k_ in range(KC):
                rms_k(k_)
            mod_emit(24)
        rms_finish()
        assert modq["l"] == l and modq["b"] == 24
        MVEC, uMVEC = MVECS[l]
        modq["l"], modq["b"] = l + 1, 0
        S.op(act, I("activation", out=LG[:], in_=CST[:, cb + C_DECF:cb + C_DECF + 16], func=AF.Exp), [uCST], [uLG])
        S.op(dve, I("tensor_scalar_mul", out=LG[:], in0=LG[:], scalar1=-1.0), [uLG], [uLG])
        for d_ in range(2):
            dst, udst = (DAF, uDAF) if d_ == 0 else (DAB, uDAB)
            pcol = A_EAF if d_ == 0 else A_EAB
            S.op(dve, I("tensor_tensor", out=AP(dst, 0, [[64, 128], [8, 8], [1, 8]]),
                                                                   in0=AP(PCA, pcol, [[NPCA, 128], [1, 8], [0, 8]]),
                                                                   in1=AP(LG, 8 * d_, [[16, 128], [0, 8], [1, 8]]), op=ALU.mult),
                 [uPCA, uLG], [udst])
            S.op(act, I("activation", out=dst[:], in_=dst[:], func=AF.Exp), [udst], [udst])
            kcol = A_KDF if d_ == 0 else A_KDB
            S.op(dve, I("tensor_scalar_mul", out=KDEC[:, 8 * d_:8 * d_ + 8], in0=LG[:, 8 * d_:8 * d_ + 8], scalar1=PCA[:, kcol:kcol + 1]),
                 [uLG, uPCA], [uKDEC])
            S.op(dve, I("tensor_scalar_mul", out=CDEC[:, 8 * d_:8 * d_ + 8], in0=LG[:, 8 * d_:8 * d_ + 8], scalar1=128.0), [uLG], [uCDEC])
        S.op(act, I("activation", out=KDEC[:], in_=KDEC[:], func=AF.Exp), [uKDEC], [uKDEC])
        S.op(act, I("activation", out=CDEC[:], in_=CDEC[:], func=AF.Exp), [uCDEC], [uCDEC])
        rta = RF(RTA_o, [[1, 128]])
        rtb = RF(RTB_o, [[1, 128]])
        for h in range(H):
            hs = slice(h * 128, (h + 1) * 128)
            S.op(act, I("activation", out=QDF[:, hs], in_=PCA[:, A_QDF:A_QDF + 128], func=AF.Exp, scale=LG[:, h:h + 1], bias=LNKS[:, 0:1]), [uPCA, uLG, uLNKS], [uQDF])
            S.op(act, I("activation", out=QDB[:, hs], in_=PCA[:, A_QDB:A_QDB + 128], func=AF.Exp, scale=LG[:, 8 + h:9 + h], bias=LNKS[:, 0:1]), [uPCA, uLG, uLNKS], [uQDB])
            S.op(act, I("activation", out=rta, in_=PCBc(B_MRF, [[1, 128]]), func=AF.Exp, scale=LG[:, h:h + 1]), [uPCB, uLG], [uRTA])
            S.op(act, I("activation", out=rtb, in_=PCBc(B_MRB, [[1, 128]]), func=AF.Exp, scale=LG[:, 8 + h:9 + h]), [uPCB, uLG], [uRTB])
            S.op(dve, I("tensor_tensor", out=rta, in0=rta, in1=PCBc(B_MU, [[1, 128]]), op=ALU.mult), [uRTA, uPCB], [uRTA])
            S.op(dve, I("tensor_tensor", out=rtb, in0=rtb, in1=PCBc(B_ML, [[1, 128]]), op=ALU.mult), [uRTB, uPCB], [uRTB])
            S.op(dve, I("tensor_tensor", out=rta, in0=rta, in1=rtb, op=ALU.add), [uRTA, uRTB], [uRTA])
            S.op(dve, I("tensor_tensor", out=MTB[:, hs], in0=rta, in1=PCBc(B_MI, [[1, 128]]), op=ALU.add), [uRTA, uPCB], [uMTB])
        for d_ in range(2):
            S.op(dve, I("tensor_tensor", out=AP(COEF, d_ * 72, [[144, 128], [8, 9], [1, 8]]),
                                                       in0=AP(PCA, A_EXC + 9 * d_, [[NPCA, 128], [1, 9], [0, 8]]),
                                                       in1=AP(LG, 8 * d_, [[16, 128], [0, 9], [1, 8]]), op=ALU.mult), [uPCA, uLG], [uCOEF])
        S.op(act, I("activation", out=COEF[:], in_=COEF[:], func=AF.Exp), [uCOEF], [uCOEF])
        S.op(dve, I("tensor_tensor", out=AP(COEF, 0, [[144, 128], [8, 18], [1, 8]]), in0=AP(COEF, 0, [[144, 128], [8, 18], [1, 8]]),
                                            in1=AP(PCA, A_MSK, [[NPCA, 128], [1, 18], [0, 8]]), op=ALU.mult), [uCOEF, uPCA], [uCOEF])

        for k in range(KC):
            for ri, (t0, t1, j) in enumerate(RANGES):
                n = t1 - t0
                (sqo, usq) = SQT[(k * 3 + ri) % 2]
                S.op(dve, I("tensor_tensor", out=RF(sqo, [[1, n]]), in0=XT[k][0][:, t0:t1], in1=RSTD[:, t0:t1], op=ALU.mult),
                     [XT[k][1], uRSTD], [usq])
                S.op(act, I("activation", out=RB(HX[k][0] + t0, [[1, n]]), in_=RF(sqo, [[1, n]]), func=AF.Identity,
                                                                                    bias=MVEC[:, mvec(j, 1) + k:mvec(j, 1) + k + 1],
                                                                                    scale=MVEC[:, mvec(j, 0) + k:mvec(j, 0) + k + 1]),
                     [usq, uMVEC], [HX[k][1]])

        def HXv(k, t0, t1, HX=HX):
            return RB(HX[k][0] + t0, [[1, t1 - t0]])

        def tok_block(kind, bi, wt, uwt):
            pend = []
            for tt in range(NT if (l < L - 1 or kind != "q") else 8):
                pb, upb = next_pb()
                for k in range(KC):
                    S.mm(I("matmul", pb[:, 0:256], HXv(k, tt * 128, (tt + 1) * 128), wt[:, k * 256:(k + 1) * 256], start=(k == 0), stop=(k == KC - 1)),
                         [HX[k][1], uwt], [upb], inc=(k == KC - 1))
                (sto, ust) = STG[tt % 4]
                if kind == "v":
                    S.op(act, I("activation", out=RB(sto, [[1, 256]]), in_=pb[:, 0:256], func=AF.Copy), [upb], [ust])
                    dma(V_d[tt][:, bi * 256:(bi + 1) * 256], RB(sto, [[1, 256]]), [ust], [uV[tt]])
                    continue
                ro = B_ROPE + tt * 192
                xv = AP(pb, 0, [[512, 128], [4, 64], [1, 4]])
                xs = AP(pb, 1, [[512, 128], [4, 64], [2, 2], [-1, 2]])
                cc_ = PCBc(ro, [[1, 64], [0, 4]])
                ss_ = PCBc(ro + 64, [[2, 64], [0, 2], [1, 2]])
                S.op(dve, I("tensor_tensor", out=RF(RTA_o, [[4, 64], [1, 4]]), in0=xv, in1=cc_, op=ALU.mult), [upb, uPCB], [uRTA])
                S.op(dve, I("tensor_tensor", out=RF(RTB_o, [[4, 64], [2, 2], [1, 2]]), in0=xs, in1=ss_, op=ALU.mult), [upb, uPCB], [uRTB])
                S.op(pool, I("tensor_tensor", out=RB(sto, [[2, 64], [128, 2], [1, 2]]), in0=RF(RTA_o, [[4, 64], [2, 2], [1, 2]]), in1=RF(RTB_o, [[4, 64], [2, 2], [1, 2]]), op=ALU.add), [uRTA, uRTB], [ust])
                if kind == "k":
                    dma(KTOK_d[tt][:, bi * 256:(bi + 1) * 256], RB(sto, [[1, 256]]), [ust], [uKTOK[tt]])
                def do_tr(tt=tt, sto=sto, ust=ust):
                    ptb = PB[7][0][:].bitcast(BF16)
                    for hh in range(2):
                        S.mm(I("transpose", ptb[:, hh * 128:(hh + 1) * 128], RB(sto + hh * 128, [[1, 128]]), IDB[:]),
                             [ust, uIDB], [PB[7][1]], inc=True)
                        S.op(act, I("activation", out=RB(STT[hh][0] + tt * 128, [[1, 128]]), in_=ptb[:, hh * 128:(hh + 1) * 128], func=AF.Copy),
                             [PB[7][1]], [STT[hh][1]])
                pend.append(do_tr)
                if len(pend) > 1:
                    pend.pop(0)()
            for f_ in pend:
                f_()
            if kind != "v":
                dst_d, udst = (KT_d, uKT) if kind == "k" else (QT_d, uQT)
                for hh in range(2):
                    h = bi * 2 + hh
                    dma(dst_d[h], RB(STT[hh][0], [[1, TT]]), [STT[hh][1]], [udst[h]])

        fm_alt = {"n": 0}

        FRANGES = RANGES if l < L - 1 else RANGES[:2]

        def fm_group(wt, uwt, half):
            banks = (PB[0], PB[1], PB[2]) if fm_alt["n"] % 2 == 0 else (PB[3], PB[4], PB[5])
            fm_alt["n"] += 1
            for ri, (t0, t1, _) in enumerate(FRANGES):
                n = t1 - t0
                for k in range(KC):
                    S.mm(I("matmul", banks[ri][0][:, 0:n], wt[:, k * 256 + half * 128:k * 256 + half * 128 + 128], HXv(k, t0, t1), start=(k == 0), stop=(k == KC - 1)),
                         [HX[k][1], uwt], [banks[ri][1]], inc=(k == KC - 1))
            return banks

        evrot = {"n": 0}

        def fm_evac(banks, g, cc):
            outs = []
            for ri, (t0, t1, _) in enumerate(FRANGES):
                n = t1 - t0
                (sto, ust) = STG[evrot["n"] % 4]
                evrot["n"] += 1
                if g == 4:
                    S.op(act, I("activation", out=RB(sto, [[1, n]]), in_=banks[ri][0][:, 0:n], func=AF.Silu), [banks[ri][1]], [ust])
                elif evrot["n"] % 2 == 0:
                    S.op(act, I("activation", out=RB(sto, [[1, n]]), in_=banks[ri][0][:, 0:n], func=AF.Copy), [banks[ri][1]], [ust])
                else:
                    S.op(dve, I("tensor_copy", out=RB(sto, [[1, n]]), in_=banks[ri][0][:, 0:n]), [banks[ri][1]], [ust])
                dma(UF_d[g, cc][:, t0:t1], RB(sto, [[1, n]]), [ust], [uUF[g][cc]])
                outs.append((sto, ust))
            return outs

        for bi_, (kind, idx) in enumerate(BLOCKS):
            if stop == l and kind in SKIP_KINDS:
                continue
            wt, uwt = w_next()
            if kind in ("q", "k", "v"):
                tok_block(kind, idx, wt, uwt)
            elif kind == "zr":
                for half in range(2):
                    banks = fm_group(wt, uwt, half)
                    fm_evac(banks, 4, idx * 2 + half)
            elif kind == "hc":
                cc = idx
                banks = fm_group(wt, uwt, 0)
                outs = fm_evac(banks, 0, cc)
                S.op(pool, I("tensor_copy", out=HBND[:, cc:cc + 1], in_=RB(outs[0][0], [[1, 1]])), [outs[0][1]], [uHBND])
                S.op(pool, I("tensor_copy", out=HBND[:, 8 + cc:9 + cc], in_=RB(outs[1][0] + 511, [[1, 1]])), [outs[1][1]], [uHBND])
                banks = fm_group(wt, uwt, 1)
                outs = fm_evac(banks, 1, cc)
                S.op(pool, I("tensor_tensor", out=HALO[:, cc:cc + 1], in0=RB(outs[0][0], [[1, 1]]), in1=HBND[:, cc:cc + 1], op=ALU.mult), [outs[0][1], uHBND], [uHALO])
                S.op(pool, I("tensor_tensor", out=HALO[:, 8 + cc:9 + cc], in0=RB(outs[1][0] + 511, [[1, 1]]), in1=HBND[:, 8 + cc:9 + cc], op=ALU.mult), [outs[1][1], uHBND], [uHALO])
            elif kind == "bz":
                cc = idx
                banks = fm_group(wt, uwt, 0)
                fm_evac(banks, 2, cc)
                banks = fm_group(wt, uwt, 1)
                fm_evac(banks, 3, cc)

        old = r_reset()
        YT = [r_b16(TT, "YT%d" % k) for k in range(8)]
        bKTOK = [r_b16(1024, "bKTOK%d" % i) for i in range(2)]
        bV = [r_b16(1024, "bV%d" % i) for i in range(2)]
        bQ = [r_b16(1024, "bQ%d" % i) for i in range(2)]
        bK = [r_b16(1024, "bK%d" % i) for i in range(2)]
        bGZ = [r_b16(1024, "bGZ%d" % i) for i in range(1)]
        bKS = r_b16(1024, "bKS")
        bPM = r_b16(1024, "bPM")
        bQF = r_b16(1024, "bQF")
        bQB = r_b16(1024, "bQB")
        bSFB = r_b16(1024, "bSFB")
        bSBS = [r_b16(1024, "bSBS%d" % i) for i in range(2)]
        bS = r_f32(1024, "bS")
        xs_mark = reg["off"]
        bT1 = r_f32(1024, "bT1")
        bT2 = r_f32(1024, "bT2")
        bPAD = r_f32(64, "bPAD")
        ret_end = reg["off"]
        XS_o = xs_mark // 4
        assert xs_mark + XW * 4 <= ret_end
        uXS = [bT1[1], bT2[1], bPAD[1]]
        barrier(old, reg["units"])

        def YTv(k, t0=0, t1=TT, YT=YT):
            return RB(YT[k][0] + t0, [[1, t1 - t0]])

        for tt in range(8):
            (ko, uk) = bKTOK[tt % 2]
            (vo, uv) = bV[tt % 2]
            dma(RB(ko, [[1, 1024]]), KTOK_d[tt], [uKTOK[tt]], [uk])
            dma(RB(vo, [[1, 1024]]), V_d[tt], [uV[tt]], [uv])
            for d_ in range(2):
                tab = DAF if d_ == 0 else DAB
                utab = uDAF if d_ == 0 else uDAB
                (kso, uks) = bKS if d_ == 0 else bPM
                S.op(dve, I("tensor_tensor", out=RB(kso, [[128, 8], [1, 128]]), in0=RB(ko, [[128, 8], [1, 128]]),
                                                 in1=AP(tab, tt * 8, [[64, 128], [1, 8], [0, 128]]), op=ALU.mult),
                     [uk, utab], [uks])
                for h in range(H):
                    pbk = PB[2 * d_ + h // 4]
                    S.mm(I("matmul", pbk[0][:, (h % 4) * 128:(h % 4) * 128 + 128], RB(kso + h * 128, [[1, 128]]), RB(vo + h * 128, [[1, 128]]), start=(tt == 0 and h % 4 == 0), stop=(tt == 7), skip_group_check=True),
                         [uks, uv], [pbk[1]], inc=(h % 4 == 3))
        for d_ in range(2):
            for hb in range(2):
                pbk = PB[2 * d_ + hb]
                S.op(act, I("activation", out=RF(XS_o + d_ * 1024 + hb * 512, [[1, 512]]), in_=pbk[0][:], func=AF.Copy), [pbk[1]], uXS)
        S.op(dve, I("tensor_copy", out=RF(XS_o + 2048, [[1, 16]]), in_=HALO[:]), [uHALO], uXS)

        if MODE == "host":
            if stop == l:
                dma(xch_out_d, RF(XS_o, [[1, XW]]), uXS, [])
                completed = False
                break
            gsrc = gin_d[l]
            ug = uXCH[l]
        else:
            dma(xch_loc_d[l], RF(XS_o, [[1, XW]]), uXS, [uXCH[l]])
            S.op(pool, I("collective_compute", "AllGather", ALU.bypass, replica_groups=[list(range(NCORE))],
                                                          ins=[xch_loc_d[l]], outs=[xch_all_d[l]]), [uXCH[l]], [uXCH[l]], dma=True, inc=1)
            gsrc = xch_all_d[l].rearrange("(r p) w -> r p w", p=128)
            ug = uXCH[l]

        dma(AP(HG, 0, [[NCORE * 16, 128], [16, NCORE], [1, 16]]), gsrc[:, :, 2048:2064].rearrange("r p w -> p r w"), [ug] + uXS, [uHG])
        S.op(dve, I("tensor_tensor", out=AP(SML, 0, [[64, 128], [8, 8], [1, 8]]), in0=AP(HG, 8, [[128, 128], [1, 8], [16, 8]]),
                                            in1=AP(PCA, A_SELP, [[NPCA, 128], [0, 8], [1, 8]]), op=ALU.mult), [uHG, uPCA, uSML], [uSML])
        S.op(dve, I("tensor_reduce", out=HALO[:, 0:8], in_=AP(SML, 0, [[64, 128], [8, 8], [1, 8]]), axis=AX.X, op=ALU.add), [uSML, uHALO], [uHALO])
        S.op(dve, I("tensor_tensor", out=AP(SML, 0, [[64, 128], [8, 8], [1, 8]]), in0=AP(HG, 0, [[128, 128], [1, 8], [16, 8]]),
                                            in1=AP(PCA, A_SELN, [[NPCA, 128], [0, 8], [1, 8]]), op=ALU.mult), [uHG, uPCA, uSML], [uSML])
        S.op(dve, I("tensor_reduce", out=HALO[:, 8:16], in_=AP(SML, 0, [[64, 128], [8, 8], [1, 8]]), axis=AX.X, op=ALU.add), [uSML, uHALO], [uHALO])

        (so_, us_) = bS
        (t1o, ut1) = bT1
        (t2o, ut2) = bT2

        def combine(d_):
            sv = RF(so_, [[128, 8], [1, 128]])
            S.op(dve, I("tensor_tensor", out=sv, in0=sv, in1=AP(COEF, d_ * 72 + 64, [[144, 128], [1, 8], [0, 128]]), op=ALU.mult), [us_, uCOEF], [us_])
            for i in range(NCORE):
                (ao, ua) = bT2
                dma(RF(ao, [[1, 1024]]), gsrc[i][:, d_ * 1024:(d_ + 1) * 1024], [ug], [ua])
                S.op(pool, I("tensor_tensor", out=RF(t1o, [[128, 8], [1, 128]]), in0=RF(ao, [[128, 8], [1, 128]]),
                                                                   in1=AP(COEF, d_ * 72 + i * 8, [[144, 128], [1, 8], [0, 128]]), op=ALU.mult), [ua, uCOEF], [ut1])
                S.op(dve, I("tensor_tensor", out=RF(so_, [[1, 1024]]), in0=RF(so_, [[1, 1024]]), in1=RF(t1o, [[1, 1024]]), op=ALU.add), [us_, ut1], [us_])

        def load_kv(tt, slot):
            (ko, uk) = bKTOK[slot]
            (vo, uv) = bV[slot]
            dma(RB(ko, [[1, 1024]]), KTOK_d[tt], [uKTOK[tt]], [uk])
            dma(RB(vo, [[1, 1024]]), V_d[tt], [uV[tt]], [uv])
            return ko, uk, vo, uv

        def kv_update(d_, ko, uk, vo, uv):
            (kso, uks) = bKS
            S.op(dve, I("tensor_tensor", out=RB(kso, [[128, 8], [1, 128]]), in0=RB(ko, [[128, 8], [1, 128]]),
                                                                in1=AP(KDEC, 8 * d_, [[16, 128], [1, 8], [0, 128]]), op=ALU.mult), [uk, uKDEC], [uks])
            for h in range(H):
                pbk = PB[6 + h // 4]
                S.mm(I("matmul", pbk[0][:, (h % 4) * 128:(h % 4) * 128 + 128], RB(kso + h * 128, [[1, 128]]), RB(vo + h * 128, [[1, 128]]), start=True, stop=True),
                     [uks, uv], [pbk[1]], inc=(h % 4 == 3))
            sv = RF(so_, [[128, 8], [1, 128]])
            S.op(pool, I("tensor_tensor", out=sv, in0=sv, in1=AP(CDEC, 8 * d_, [[16, 128], [1, 8], [0, 128]]), op=ALU.mult), [us_, uCDEC], [us_])
            for hb in range(2):
                S.op(dve, I("tensor_tensor", out=RF(so_ + hb * 512, [[1, 512]]), in0=RF(so_ + hb * 512, [[1, 512]]), in1=PB[6 + hb][0][:], op=ALU.add),
                     [us_, PB[6 + hb][1]], [us_])

        S.op(pool, I("memset", RF(so_, [[1, 1024]]), 0.0), [], [us_])
        border = [9, 8] + list(range(7, -1, -1))
        kbufs = (bKS, bPM)
        pbsets = (6, 4)

        def kv_pre(d_, tt, par):
            ko, uk, vo, uv = load_kv(tt, par)
            (kso, uks) = kbufs[par]
            S.op(dve, I("tensor_tensor", out=RB(kso, [[128, 8], [1, 128]]), in0=RB(ko, [[128, 8], [1, 128]]),
                        in1=AP(KDEC, 8 * d_, [[16, 128], [1, 8], [0, 128]]), op=ALU.mult), [uk, uKDEC], [uks])
            for h in range(H):
                pbk = PB[pbsets[par] + h // 4]
                S.mm(I("matmul", pbk[0][:, (h % 4) * 128:(h % 4) * 128 + 128], RB(kso + h * 128, [[1, 128]]), RB(vo + h * 128, [[1, 128]]), start=True, stop=True),
                     [uks, uv], [pbk[1]], inc=(h % 4 == 3))

        def s_upd(d_, par):
            sv = RF(so_, [[128, 8], [1, 128]])
            S.op(pool, I("tensor_tensor", out=sv, in0=sv, in1=AP(CDEC, 8 * d_, [[16, 128], [1, 8], [0, 128]]), op=ALU.mult), [us_, uCDEC], [us_])
            for hb in range(2):
                pbk = PB[pbsets[par] + hb]
                S.op(dve, I("tensor_tensor", out=RF(so_ + hb * 512, [[1, 512]]), in0=RF(so_ + hb * 512, [[1, 512]]), in1=pbk[0][:], op=ALU.add),
                     [us_, pbk[1]], [us_])

        kv_pre(1, border[0], 0)
        for n_, tt in enumerate(border):
            if tt == 7:
                combine(1)
            (so, uso) = bSBS[n_ % 2]
            S.op(act, I("activation", out=RB(so, [[1, 1024]]), in_=RF(so_, [[1, 1024]]), func=AF.Copy), [us_], [uso])
            dma(SBS_d[tt], RB(so, [[1, 1024]]), [uso], [uSBSd[tt]])
            if n_ + 1 < len(border) and border[n_ + 1] != 0:
                kv_pre(1, border[n_ + 1], (n_ + 1) % 2)
            if tt != 0:
                s_upd(1, n_ % 2)
            mod_emit(1)
        S.op(pool, I("memset", RF(so_, [[1, 1024]]), 0.0), [], [us_])
        forder = [8, 9] + list(range(8))
        def fwd_A(n_, tt):
            slot = n_ % 2
            if tt == 0:
                combine(0)
            (sfb, usfb) = bSFB
            S.op(act, I("activation", out=RB(sfb, [[1, 1024]]), in_=RF(so_, [[1, 1024]]), func=AF.Copy), [us_], [usfb])
            ko, uk, vo, uv = load_kv(tt, slot)
            (qo, uq) = bQ[slot]
            (kto, ukt) = bK[slot]
            (sbs, usbs) = bSBS[slot]
            tsl = slice(tt * 128, (tt + 1) * 128)
            dma(RB(qo, [[128, 8], [1, 128]]), QT_d[:, :, tsl].rearrange("h d t -> d h t"), uQT, [uq])
            dma(RB(kto, [[128, 8], [1, 128]]), KT_d[:, :, tsl].rearrange("h d t -> d h t"), uKT, [ukt])
            dma(RB(sbs, [[1, 1024]]), SBS_d[tt], [uSBSd[tt]], [usbs])
            for h in range(H):
                pbk = PB[h // 4]
                S.mm(I("matmul", pbk[0][:, (h % 4) * 128:(h % 4) * 128 + 128], RB(kto + h * 128, [[1, 128]]), RB(qo + h * 128, [[1, 128]]), start=True, stop=True),
                     [ukt, uq], [pbk[1]], inc=(h % 4 == 3))
            (pmo, upm) = bPM
            for hb in range(2):
                S.op(dve, I("tensor_tensor", out=RB(pmo + hb * 512, [[1, 512]]), in0=PB[hb][0][:], in1=MTB[:, hb * 512:(hb + 1) * 512], op=ALU.mult),
                     [PB[hb][1], uMTB], [upm])
            (qfo, uqf) = bQF
            (qbo, uqb) = bQB
            S.op(pool, I("tensor_tensor", out=RB(qfo, [[1, 1024]]), in0=RB(qo, [[1, 1024]]), in1=QDF[:], op=ALU.mult), [uq, uQDF], [uqf])
            S.op(pool, I("tensor_tensor", out=RB(qbo, [[1, 1024]]), in0=RB(qo, [[1, 1024]]), in1=QDB[:], op=ALU.mult), [uq, uQDB], [uqb])
            for h in range(H):
                pbk = PB[2 + h // 4]
                osl = slice((h % 4) * 128, (h % 4) * 128 + 128)
                S.mm(I("matmul", pbk[0][:, osl], RB(vo + h * 128, [[1, 128]]), RB(pmo + h * 128, [[1, 128]]), start=True, stop=False),
                     [uv, upm], [pbk[1]], inc=False)
                S.mm(I("matmul", pbk[0][:, osl], RB(sfb + h * 128, [[1, 128]]), RB(qfo + h * 128, [[1, 128]]), start=False, stop=False),
                     [usfb, uqf], [pbk[1]], inc=False)
                S.mm(I("matmul", pbk[0][:, osl], RB(sbs + h * 128, [[1, 128]]), RB(qbo + h * 128, [[1, 128]]), start=False, stop=True),
                     [usbs, uqb], [pbk[1]], inc=True)
            for hb in range(2):
                S.op(act, I("activation", out=GN1[:, hb * 512:(hb + 1) * 512], in_=PB[2 + hb][0][:], func=AF.Copy), [PB[2 + hb][1]], [uGN1])
            if tt != 7:
                kv_update(0, ko, uk, vo, uv)

        def fwd_B1(tt):
            for hb in range(2):
                S.mm(I("matmul", PB[4 + hb][0][:], ONESM[:], GN1[:, hb * 512:(hb + 1) * 512], start=True, stop=True), [uONESM, uGN1], [PB[4 + hb][1]], inc=True)
                S.op(dve, I("tensor_tensor", out=RF(t2o + hb * 512, [[1, 512]]), in0=GN1[:, hb * 512:(hb + 1) * 512].bitcast(F32), in1=PB[4 + hb][0][:], op=ALU.subtract),
                     [uGN1, PB[4 + hb][1]], [ut2])

        def fwd_B2(tt):
            (gzo, ugz) = bGZ[0]
            tsl = slice(tt * 128, (tt + 1) * 128)
            dma(RB(gzo, [[128, 8], [1, 128]]), UF_d[4][:, :, tsl].rearrange("h d t -> d h t"), uUF[4], [ugz])
            for hb in range(2):
                S.op(act, I("activation", out=GN3[:, hb * 512:(hb + 1) * 512], in_=RF(t2o + hb * 512, [[1, 512]]), func=AF.Square), [ut2], [uGN3])
                S.mm(I("matmul", PB[4 + hb][0][:], ONESM[:], GN3[:, hb * 512:(hb + 1) * 512], start=True, stop=True), [uONESM, uGN3], [PB[4 + hb][1]], inc=True)
            for hb in range(2):
                S.op(act, I("activation", out=RF(t1o + hb * 512, [[1, 512]]), in_=PB[4 + hb][0][:], func=AF.Ln, bias=EPSC[:, 0:1], scale=1.0), [PB[4 + hb][1], uEPSC], [ut1])
            S.op(act, I("activation", out=RF(t1o, [[1, 1024]]), in_=RF(t1o, [[1, 1024]]), func=AF.Exp, scale=-0.5), [ut1], [ut1])
            S.op(pool, I("tensor_tensor", out=RF(t2o, [[1, 1024]]), in0=RF(t2o, [[1, 1024]]), in1=RF(t1o, [[1, 1024]]), op=ALU.mult), [ut1, ut2], [ut2])
            S.op(pool, I("tensor_tensor", out=RF(t2o, [[128, 8], [1, 128]]), in0=RF(t2o, [[128, 8], [1, 128]]),
                         in1=AP(CST, cb + C_RNW, [[NCST, 128], [1, 8], [0, 128]]), op=ALU.mult), [ut2, uCST], [ut2])
            for h in range(H):
                S.op(dve if h % 2 == 0 else pool, I("tensor_tensor", out=YTv(h, tt * 128, tt * 128 + 128), in0=RF(t2o + h * 128, [[1, 128]]), in1=RB(gzo + h * 128, [[1, 128]]), op=ALU.mult),
                     [ut2, ugz], [YT[h][1]])

        pend_b2 = None
        for n_, tt in enumerate(forder):
            if l == L - 1 and tt >= 8:
                ko, uk, vo, uv = load_kv(tt, n_ % 2)
                kv_update(0, ko, uk, vo, uv)
                continue
            if tt == 0 and pend_b2 is not None:
                fwd_B2(pend_b2)
                pend_b2 = None
            fwd_A(n_, tt)
            if pend_b2 is not None:
                fwd_B2(pend_b2)
            fwd_B1(tt)
            pend_b2 = tt
            mod_emit(2)
        fwd_B2(pend_b2)

        def out_proj(part):
            for b in range(8):
                wt, uwt = w_next()
                for half in range(2):
                    dmc = b * 2 + half
                    for ri, (t0, t1, j) in enumerate(RANGES):
                        if l == L - 1 and j == 1:
                            continue
                        n = t1 - t0
                        pb, upb = next_pb(0, 5)
                        for k in range(8):
                            S.mm(I("matmul", pb[:, 0:n], wt[:, k * 256 + half * 128:k * 256 + half * 128 + 128], YTv(k, t0, t1), start=(k == 0), stop=(k == 7)),
                                 [uwt, YT[k][1]], [upb], inc=(k == 7))
                        S.op(dve, I("scalar_tensor_tensor", out=XT[dmc][0][:, t0:t1], in0=pb[:, 0:n], scalar=MVEC[:, mvec(j, 2) + dmc:mvec(j, 2) + dmc + 1], in1=XT[dmc][0][:, t0:t1], op0=ALU.mult, op1=ALU.add),
                             [upb, uMVEC, XT[dmc][1]], [XT[dmc][1]])
                    if part == 0:
                        if dmc >= 3:
                            rms_k(dmc - 3)
            if part == 0:
                for k_ in range(KC - 3, KC):
                    rms_k(k_)

        mod_emit(24)
        out_proj(1)

        old_all = list(reg["units"])
        keep = [u for (_, u) in YT]
        reg["off"] = YT[7][0] * 2 + ((TT * 2 + 63) // 64) * 64
        reg["units"] = list(keep)
        cin = [[r_b16(TT, "cin%d_%d" % (g, i)) for i in range(2 if g < 2 else 1)] for g in range(4)]
        CH_o, uCH = r_f32(TT + 8, "CH")
        CV_o, uCV = r_f32(TT + 8, "CV")
        Yb = [r_f32(TT, "Y%d" % i) for i in range(1)]
        SZ_o, uSZ = r_f32(TT, "SZ")
        SSC2_o, uSSC2 = r_f32(TT, "SSC2")
        barrier(old_all, reg["units"])
        S.op(pool, I("memset", RF(CH_o, [[1, TT + 8]]), 0.0), [], [uCH])
        cw = cb + C_CONVW
        for cc in range(8):
            ins_ = []
            for g in range(4):
                (o, u) = cin[g][cc % len(cin[g])]
                dma(RB(o, [[1, TT]]), UF_d[g, cc], [uUF[g][cc]], [u])
                ins_.append((o, u))
            (ho, uh), (co, uc), (bo, ub), (zo, uz) = ins_
            S.op(pool, I("tensor_copy", out=RF(CH_o, [[1, 1]]), in_=HALO[:, cc:cc + 1]), [uHALO, uCH], [uCH])
            S.op(pool, I("tensor_copy", out=RF(CH_o + 1025, [[1, 1]]), in_=HALO[:, 8 + cc:9 + cc]), [uHALO, uCH], [uCH])
            S.op(dve, I("tensor_tensor", out=RF(CH_o + 1, [[1, T]]), in0=RB(ho, [[1, T]]), in1=RB(co, [[1, T]]), op=ALU.mult), [uh, uc, uCH], [uCH])
            S.op(dve, I("tensor_tensor", out=RF(CH_o + 1027, [[1, TC]]), in0=RB(ho + T, [[1, TC]]), in1=RB(co + T, [[1, TC]]), op=ALU.mult), [uh, uc, uCH], [uCH])
            for (c0, n, o0) in ((1, T, 0), (1027, TC, T)):
                S.op(act, I("activation", out=RF(CV_o + o0, [[1, n]]), in_=RF(CH_o + c0 - 1, [[1, n]]), func=AF.Identity, scale=CST[:, cw + cc:cw + cc + 1]), [uCH, uCST, uCV], [uCV])
                S.op(dve, I("scalar_tensor_tensor", out=RF(CV_o + o0, [[1, n]]), in0=RF(CH_o + c0, [[1, n]]), scalar=CST[:, cw + 8 + cc:cw + 9 + cc], in1=RF(CV_o + o0, [[1, n]]), op0=ALU.mult, op1=ALU.add), [uCH, uCST, uCV], [uCV])
                S.op(dve, I("scalar_tensor_tensor", out=RF(CV_o + o0, [[1, n]]), in0=RF(CH_o + c0 + 1, [[1, n]]), scalar=CST[:, cw + 16 + cc:cw + 17 + cc], in1=RF(CV_o + o0, [[1, n]]), op0=ALU.mult, op1=ALU.add), [uCH, uCST, uCV], [uCV])
            (yo, uy) = Yb[0]
            S.op(dve, I("tensor_tensor", out=RF(yo, [[1, TT]]), in0=RF(CV_o, [[1, TT]]), in1=RB(bo, [[1, TT]]), op=ALU.mult), [uCV, ub], [uy])
            for ri, (t0, t1, _) in enumerate(RANGES):
                n = t1 - t0
                (sq, usq) = SQR[(cc * 3 + ri) % 2]
                S.op(act, I("activation", out=sq[:, 0:n], in_=RF(yo + t0, [[1, n]]), func=AF.Square), [uy], [usq])
                S.mm(I("matmul", PB[ri][0][:, 0:n], ONES[:], sq[:, 0:n], start=(cc == 0), stop=(cc == 7)), [uONES, usq], [PB[ri][1]], inc=True)
            S.op(act, I("activation", out=RF(SZ_o, [[1, TT]]), in_=RB(zo, [[1, TT]]), func=AF.Silu), [uz], [uSZ])
            S.op(dve, I("scalar_tensor_tensor", out=YTv(cc), in0=RF(yo, [[1, TT]]), scalar=CST[:, cb + C_CNW + cc:cb + C_CNW + cc + 1], in1=RF(SZ_o, [[1, TT]]), op0=ALU.mult, op1=ALU.mult), [uy, uSZ, uCST], [YT[cc][1]])
        for ri, (t0, t1, _) in enumerate(RANGES):
            n = t1 - t0
            S.op(act, I("activation", out=RSTD[:, t0:t1], in_=PB[ri][0][:, 0:n], func=AF.Ln, bias=EPSC[:, 0:1], scale=1.0 / 1024.0), [PB[ri][1], uEPSC], [uRSTD])
        S.op(act, I("activation", out=RSTD[:], in_=RSTD[:], func=AF.Exp, scale=-0.5), [uRSTD], [uRSTD])
        for cc in range(8):
            S.op(dve if cc % 2 == 0 else pool, I("tensor_tensor", out=YTv(cc), in0=YTv(cc), in1=RSTD[:], op=ALU.mult), [YT[cc][1], uRSTD], [YT[cc][1]])
        out_proj(0)
        if DBG and l == 0:
            dbg_d = nc.dram_tensor("dbgx", [D, TT], F32, kind="ExternalOutput").ap()
            for k in range(KC):
                dma(dbg_d[k * 128:(k + 1) * 128, :], XT[k][0][:], [XT[k][1]], [])

    if completed:
        old = r_reset()
        OST = [r_f32(512, "OST%d" % i) for i in range(2)]
        barrier(old, reg["units"])
        rms_finish()
        for k in range(KC):
            for ri, (t0, t1, j) in enumerate(RANGES[:2]):
                n = t1 - t0
                (oo, uo) = OST[(k * 2 + ri) % 2]
                S.op(dve, I("scalar_tensor_tensor", out=RF(oo, [[1, n]]), in0=XT[k][0][:, t0:t1], scalar=CST[:, C_FNW + k:C_FNW + k + 1], in1=RSTD[:, t0:t1], op0=ALU.mult, op1=ALU.mult),
                     [XT[k][1], uRSTD, uCST], [uo])
                dma(out_d[k * 128:(k + 1) * 128, t0:t1], RF(oo, [[1, n]]), [uo], [])
    S.final_wait_all()
    S.replay()
    return nc


def _pretile(w, cols_per_block, kc=KC):
    out = []
    for cols in cols_per_block:
        blk = w[:, cols]
        blk = blk.reshape(kc, 128, 256).transpose(1, 0, 2).reshape(128, kc * 256)
        out.append(blk)
    return np.ascontiguousarray(np.stack(out, 0))


def _fm(v):
    return np.ascontiguousarray(v.reshape(-1, 128).T)


def _host_prep(x, c, ctx, c_ctx, norm_w, w_mod, b_mod, w_in, conv_w, conv_norm_w, ret_norm_w,
               ret_decay_f, ret_decay_b, w_out, final_norm_w):
    f32 = np.float32
    x = np.asarray(x, f32); ctx = np.asarray(ctx, f32)
    cst = np.zeros((128, NCST), f32)
    for l in range(L):
        cb = l * CL
        cst[:, cb + C_NW:cb + C_NW + 16] = _fm(np.asarray(norm_w[l], f32))
        cst[:, cb + C_BMOD:cb + C_BMOD + 48] = _fm(np.asarray(b_mod[l], f32))
        for k in range(3):
            cst[:, cb + C_CONVW + 8 * k:cb + C_CONVW + 8 * k + 8] = _fm(np.asarray(conv_w[l][k], f32))
        cst[:, cb + C_CNW:cb + C_CNW + 8] = _fm(np.asarray(conv_norm_w[l], f32))
        cst[:, cb + C_RNW:cb + C_RNW + 8] = _fm(np.asarray(ret_norm_w[l], f32))
        cst[:, cb + C_DECF:cb + C_DECF + 8] = np.asarray(ret_decay_f[l], f32)[None, :]
        cst[:, cb + C_DECB:cb + C_DECB + 8] = np.asarray(ret_decay_b[l], f32)[None, :]
    cst[:, C_FNW:C_FNW + 16] = _fm(np.asarray(final_norm_w, f32))
    c2 = np.ascontiguousarray(np.stack([_fm(np.asarray(c, f32).reshape(-1)), _fm(np.asarray(c_ctx, f32).reshape(-1))], -1).reshape(128, KC * 2))
    b256 = lambda n: [np.arange(b * 256, (b + 1) * 256) for b in range(n)]
    wmod = np.stack([_pretile(np.asarray(w_mod[l], f32), b256(24)) for l in range(L)], 0)
    win = np.stack([_pretile(np.asarray(w_in[l], f32), [_block_cols(k, i) for (k, i) in BLOCKS]) for l in range(L)], 0)
    wout = np.stack([np.stack([_pretile(np.asarray(w_out[l], f32)[part * 1024:(part + 1) * 1024], b256(8), kc=8) for part in range(2)], 0) for l in range(L)], 0)
    ctxT = ctx[0].T
    j = np.arange(128, dtype=np.float64)
    i = np.arange(128, dtype=np.float64)
    inv = (10000.0 ** (-np.arange(32, dtype=np.float32) / np.float32(32.0))).astype(np.float32)
    ks = 128.0 ** -0.5
    maps = []
    for r in range(NCORE):
        pca = np.zeros((128, NPCA), np.float64)
        pcb = np.zeros((128, NPCB), np.float64)
        for tt in range(8):
            pca[:, A_EAF + tt] = 1023 - 128 * tt - j
            pca[:, A_EAB + tt] = 128 * tt + j
        pca[:, A_KDF] = 127 - j
        pca[:, A_KDB] = j
        pca[:, A_QDF:A_QDF + 128] = (i + 1)[None, :]
        pca[:, A_QDB:A_QDB + 128] = (128 - i)[None, :]
        dm = i[None, :] - j[:, None]
        pcb[:, B_MRF:B_MRF + 128] = np.maximum(dm, 0)
        pcb[:, B_MRB:B_MRB + 128] = np.maximum(-dm, 0)
        pcb[:, B_MU:B_MU + 128] = ks * (dm > 0)
        pcb[:, B_ML:B_ML + 128] = ks * (dm < 0)
        pcb[:, B_MI:B_MI + 128] = 2.0 * ks * (dm == 0)
        pcb[:, B_ID:B_ID + 128] = np.eye(128)
        for s in range(NCORE):
            pca[:, A_EXC + s] = 1024 * (r - 1 - s) if s < r else 0
            pca[:, A_MSK + s] = 1.0 if s < r else 0.0
            pca[:, A_EXC + 9 + s] = 1024 * (s - r - 1) if s > r else 0
            pca[:, A_MSK + 9 + s] = 1.0 if s > r else 0.0
        pca[:, A_EXC + 8] = 1024 * r
        pca[:, A_MSK + 8] = 1.0
        pca[:, A_EXC + 17] = 1024 * (NCORE - 1 - r)
        pca[:, A_MSK + 17] = 1.0
        if r > 0:
            pca[:, A_SELP + r - 1] = 1.0
        if r < NCORE - 1:
            pca[:, A_SELN + r + 1] = 1.0
        for tt in range(NT):
            base = B_ROPE + tt * 192
            if tt < 8:
                tg = r * T + tt * 128 + np.arange(128)
                rowp = (tg // 64).astype(np.float32)
                colp = (tg % 64).astype(np.float32)
                ang = np.concatenate([rowp[:, None] * inv[None, :], colp[:, None] * inv[None, :]], 1).astype(np.float64)
                cs, sn = np.cos(ang), np.sin(ang)
            else:
                cs, sn = np.ones((128, 64)), np.zeros((128, 64))
            pcb[:, base:base + 64] = cs
            ssn = np.stack([-sn, sn], 2)
            pcb[:, base + 64:base + 192] = ssn.reshape(128, 128)
        xT = np.ascontiguousarray(np.concatenate([x[0, r * T:(r + 1) * T, :].T, ctxT], 1))
        maps.append({"xT": xT, "c2": c2, "cst": cst, "pca": pca.astype(f32), "pcb": pcb.astype(f32),
                     "wmod": wmod, "win": win, "wout": wout})
    return maps


_PROGS = {}


def _prog(stop):
    if stop not in _PROGS:
        _PROGS[stop] = build_program(stop)
    return _PROGS[stop]


def kernel(**inputs):
    maps = _host_prep(**inputs)
    cores = list(range(NCORE))
    if MODE == "host":
        gins = []
        for e in range(L):
            m = [dict(mm_, **{"gin%d" % i: g for i, g in enumerate(gins)}) for mm_ in maps]
            res = run_bass_kernel_spmd(_prog(e), m, core_ids=cores)
            gins.append(np.ascontiguousarray(np.stack([np.asarray(r["xch"], np.float32) for r in res.results], 0)))
        m = [dict(mm_, **{"gin%d" % i: g for i, g in enumerate(gins)}) for mm_ in maps]
        res = run_bass_kernel_spmd(_prog(None), m, core_ids=cores)
    else:
        res = run_bass_kernel_spmd(_prog(None), maps, core_ids=cores)
    out = np.concatenate([np.asarray(r["outT"], np.float32).T for r in res.results], 0)
    return out[None].astype(np.float32)
```
